# Optimizing a Trainium2 kernel written in Bass

```python
import math
import jax
import jax.numpy as jnp
from jax import lax
import numpy as np

D_MODEL = 1024
BATCH = 4
SEQ = 4096
DEPTH = 1
DEC_BATCH = 8
DEC_SEQ = 32
PAST_LEN = 2048

CHUNK = 64
SB_HEADS = 8
SB_HEAD_DIM = 64
SB_WIDTH = SB_HEADS * SB_HEAD_DIM
SB_BLOCK = 128
RW_HEADS = 8
RW_HEAD_DIM = 64
RW_WIDTH = RW_HEADS * RW_HEAD_DIM
DECAY_LORA = 64
AAA_LORA = 64
GATE_LORA = 160
RW_SHIFT_WIDTH = 3 * RW_WIDTH + DECAY_LORA + AAA_LORA + GATE_LORA
PROJ_WIDTH = 3 * SB_WIDTH + RW_SHIFT_WIDTH
FFN_HIDDEN = -(-8 * D_MODEL // (3 * 256)) * 256
RMS_EPS = 1e-6
GN_EPS = 64e-5
DECAY_SCALE = math.exp(-0.5)

kernel_name = 'hybrid_stickbreak_rwkv7_stream_step'


def _rmsnorm(x, g):
    xf = x.astype(jnp.float32)
    y = xf * lax.rsqrt(jnp.mean(xf * xf, axis=-1, keepdims=True) + RMS_EPS)
    return (y * g.astype(jnp.float32)).astype(x.dtype)


def _sb_block(q, k, v, q_pos, k_pos):
    z = jnp.einsum('bqhd,bkhd->bhqk', q.astype(jnp.float32), k.astype(jnp.float32)) / math.sqrt(SB_HEAD_DIM)
    mask = k_pos[None, :] < q_pos[:, None]
    log_1mb = jnp.where(mask, jax.nn.log_sigmoid(-z), 0.0)
    after = lax.cumsum(log_1mb, axis=3, reverse=True) - log_1mb
    weight = jnp.where(mask, jnp.exp(jax.nn.log_sigmoid(z) + after), 0.0)
    o = jnp.einsum('bhqk,bkhd->bqhd', weight, v.astype(jnp.float32))
    return o.astype(q.dtype)


def _sb_prompt(q, k, v):
    B, T, H, D = q.shape
    nb = T // SB_BLOCK
    qb = jnp.moveaxis(q.reshape(B, nb, SB_BLOCK, H, D), 1, 0)
    pos = jnp.arange(T, dtype=jnp.int32)
    qpos = pos.reshape(nb, SB_BLOCK)
    ob = lax.map(lambda blk: _sb_block(blk[0], k, v, blk[1], pos), (qb, qpos))
    return jnp.moveaxis(ob, 0, 1).reshape(B, T, H, D)


def _rwkv7_scan(wkv0, r, log_w, k, v, kk, a):
    def step(S, inp):
        r_t, lw_t, k_t, v_t, kk_t, a_t = inp
        sa = jnp.einsum('bhvk,bhk->bhv', S, kk_t)
        S = (S * jnp.exp(lw_t)[:, :, None, :]
             - sa[..., :, None] * (kk_t * a_t)[..., None, :]
             + v_t[..., :, None] * k_t[..., None, :])
        return S, jnp.einsum('bhvk,bhk->bhv', S, r_t)
    xs = tuple(jnp.moveaxis(t, 1, 0) for t in (r, log_w, k, v, kk, a))
    S, ys = lax.scan(step, wkv0, xs)
    return S, jnp.moveaxis(ys, 0, 1)


def _layer(x, k_past, v_past, wkv0, shift0, p):
    B, T, _ = x.shape
    f32 = lambda t: t.astype(jnp.float32)
    h = _rmsnorm(x, p['norm_mix_g'])
    proj = h @ p['w_in']

    q, k, v = jnp.split(proj[..., :3 * SB_WIDTH], 3, axis=-1)
    q = q.reshape(B, T, SB_HEADS, SB_HEAD_DIM)
    k = k.reshape(B, T, SB_HEADS, SB_HEAD_DIM)
    v = v.reshape(B, T, SB_HEADS, SB_HEAD_DIM)
    if k_past is None:
        o_sb = _sb_prompt(q, k, v)
    else:
        past = k_past.shape[1]
        k_all = jnp.concatenate([k_past.astype(k.dtype), k], axis=1)
        v_all = jnp.concatenate([v_past.astype(v.dtype), v], axis=1)
        q_pos = past + jnp.arange(T, dtype=jnp.int32)
        k_pos = jnp.arange(past + T, dtype=jnp.int32)
        o_sb = _sb_block(q, k_all, v_all, q_pos, k_pos)

    xp = proj[..., 3 * SB_WIDTH:]
    prev = jnp.concatenate([shift0.astype(xp.dtype), xp[:, :-1]], axis=1)
    xs = f32(xp + (prev - xp) * p['mu_shift'])
    o1 = RW_WIDTH
    o2 = 2 * RW_WIDTH
    o3 = 3 * RW_WIDTH
    o4 = o3 + DECAY_LORA
    o5 = o4 + AAA_LORA
    r, kr, vr, dw, da, dg = jnp.split(xs, [o1, o2, o3, o4, o5], axis=-1)
    log_w = -DECAY_SCALE * jax.nn.sigmoid(f32(p['w0']) + jnp.tanh(dw) @ f32(p['w_decay_up']))
    a = jax.nn.sigmoid(f32(p['a0']) + da @ f32(p['w_aaa_up']))
    g = jax.nn.sigmoid(dg) @ f32(p['w_gate_up'])
    heads = lambda t: t.reshape(B, T, RW_HEADS, RW_HEAD_DIM)
    kk = heads(kr * f32(p['k_k']))
    kk = kk / jnp.maximum(jnp.linalg.norm(kk, axis=-1, keepdims=True), 1e-12)
    kr = kr * (1.0 + (a - 1.0) * f32(p['k_a']))
    r_h, k_h, v_h, a_h, lw_h = heads(r), heads(kr), heads(vr), heads(a), heads(log_w)
    S, o = _rwkv7_scan(f32(wkv0), r_h, lw_h, k_h, v_h, kk, a_h)
    mu = jnp.mean(o, axis=-1, keepdims=True)
    var = jnp.mean(jnp.square(o - mu), axis=-1, keepdims=True)
    o = ((o - mu) * lax.rsqrt(var + GN_EPS)).reshape(B, T, RW_WIDTH) * f32(p['ln_x_w']) + f32(p['ln_x_b'])
    bonus = jnp.sum(r_h * k_h * f32(p['r_k']), axis=-1, keepdims=True) * v_h
    o_rw = (o + bonus.reshape(B, T, RW_WIDTH)) * g

    mix = jnp.concatenate([o_sb.reshape(B, T, SB_WIDTH).astype(x.dtype), o_rw.astype(x.dtype)], axis=-1)
    x = x + mix @ p['w_out']
    h2 = _rmsnorm(x, p['norm_ffn_g'])
    x = x + (jax.nn.silu(h2 @ p['w_gate']) * (h2 @ p['w_up'])) @ p['w_down']
    return x, k, v, S.astype(x.dtype), xp[:, -1:]


def setup_inputs(seed: int = 0) -> dict:
    key = jax.random.key(seed)
    ks = jax.random.split(key, 26)
    nrm = lambda i, shape, scale: scale * jax.random.normal(ks[i], shape, jnp.float32)
    L = DEPTH
    return {
        'x_prompt': nrm(0, (BATCH, SEQ, D_MODEL), 1.0),
        'x_sample': nrm(1, (DEC_BATCH, DEC_SEQ, D_MODEL), 1.0),
        'cache_k': nrm(2, (L, DEC_BATCH, PAST_LEN, SB_HEADS, SB_HEAD_DIM), 1.0),
        'cache_v': nrm(3, (L, DEC_BATCH, PAST_LEN, SB_HEADS, SB_HEAD_DIM), 1.0),
        'state_wkv': nrm(4, (L, DEC_BATCH, RW_HEADS, RW_HEAD_DIM, RW_HEAD_DIM), 0.3),
        'state_shift': nrm(5, (L, DEC_BATCH, 1, RW_SHIFT_WIDTH), 1.0),
        'norm_mix_g': 1.0 + nrm(6, (L, D_MODEL), 0.02),
        'w_in': nrm(7, (L, D_MODEL, PROJ_WIDTH), D_MODEL ** -0.5),
        'mu_shift': jax.random.uniform(ks[8], (L, RW_SHIFT_WIDTH), jnp.float32),
        'w0': nrm(9, (L, RW_WIDTH), 0.5),
        'w_decay_up': nrm(10, (L, DECAY_LORA, RW_WIDTH), 0.1 * DECAY_LORA ** -0.5),
        'a0': nrm(11, (L, RW_WIDTH), 0.1),
        'w_aaa_up': nrm(12, (L, AAA_LORA, RW_WIDTH), 0.1 * AAA_LORA ** -0.5),
        'w_gate_up': nrm(13, (L, GATE_LORA, RW_WIDTH), GATE_LORA ** -0.5),
        'k_k': 0.85 + nrm(14, (L, RW_WIDTH), 0.02),
        'k_a': 1.0 + nrm(15, (L, RW_WIDTH), 0.02),
        'r_k': nrm(16, (L, RW_HEADS, RW_HEAD_DIM), 0.1),
        'ln_x_w': 1.0 + nrm(17, (L, RW_WIDTH), 0.02),
        'ln_x_b': nrm(18, (L, RW_WIDTH), 0.02),
        'w_out': nrm(19, (L, D_MODEL, D_MODEL), D_MODEL ** -0.5),
        'norm_ffn_g': 1.0 + nrm(20, (L, D_MODEL), 0.02),
        'w_gate': nrm(21, (L, D_MODEL, FFN_HIDDEN), D_MODEL ** -0.5),
        'w_up': nrm(22, (L, D_MODEL, FFN_HIDDEN), D_MODEL ** -0.5),
        'w_down': nrm(23, (L, FFN_HIDDEN, D_MODEL), FFN_HIDDEN ** -0.5),
        'norm_final_g': 1.0 + nrm(24, (D_MODEL,), 0.02),
    }


def reference(x_prompt, x_sample, cache_k, cache_v, state_wkv, state_shift,
              norm_mix_g, w_in, mu_shift, w0, w_decay_up, a0, w_aaa_up, w_gate_up,
              k_k, k_a, r_k, ln_x_w, ln_x_b, w_out, norm_ffn_g, w_gate, w_up, w_down,
              norm_final_g):
    B = x_prompt.shape[0]
    xp_, xs_ = x_prompt, x_sample
    pk, pv, pS, psh = [], [], [], []
    sk, sv, sS, ssh = [], [], [], []
    for l in range(DEPTH):
        p = {
            'norm_mix_g': norm_mix_g[l], 'w_in': w_in[l], 'mu_shift': mu_shift[l],
            'w0': w0[l], 'w_decay_up': w_decay_up[l], 'a0': a0[l], 'w_aaa_up': w_aaa_up[l],
            'w_gate_up': w_gate_up[l], 'k_k': k_k[l], 'k_a': k_a[l], 'r_k': r_k[l],
            'ln_x_w': ln_x_w[l], 'ln_x_b': ln_x_b[l], 'w_out': w_out[l],
            'norm_ffn_g': norm_ffn_g[l], 'w_gate': w_gate[l], 'w_up': w_up[l], 'w_down': w_down[l],
        }
        wkv_zero = jnp.zeros((B, RW_HEADS, RW_HEAD_DIM, RW_HEAD_DIM), x_prompt.dtype)
        shift_zero = jnp.zeros((B, 1, RW_SHIFT_WIDTH), x_prompt.dtype)
        xp_, k1, v1, S1, sh1 = _layer(xp_, None, None, wkv_zero, shift_zero, p)
        xs_, k2, v2, S2, sh2 = _layer(xs_, cache_k[l], cache_v[l], state_wkv[l], state_shift[l], p)
        pk.append(k1)
        pv.append(v1)
        pS.append(S1)
        psh.append(sh1)
        sk.append(k2)
        sv.append(v2)
        sS.append(S2)
        ssh.append(sh2)
    y_prompt = _rmsnorm(xp_, norm_final_g)
    y_sample = _rmsnorm(xs_, norm_final_g)
    return (y_prompt, y_sample,
            jnp.stack(pk), jnp.stack(pv), jnp.stack(pS), jnp.stack(psh),
            jnp.stack(sk), jnp.stack(sv), jnp.stack(sS), jnp.stack(ssh))
```

```python
import contextlib
import math
import os

import numpy as np
import concourse.bass as bass
import concourse.mybir as mybir
from concourse.bass_utils import run_bass_kernel_spmd

F32 = mybir.dt.float32
BF16 = mybir.dt.bfloat16
AF = mybir.ActivationFunctionType
ALU = mybir.AluOpType

ENGS = ("pe", "act", "dve", "pool", "sp")

T = 4096
NS = 32
TT = T + 2 * NS
TH = T // 2 + NS
D = 1024
FF = 2816
NCOL = 1824
DEC = math.exp(-0.5)


class Prog:
    def __init__(self, nc, sync_same_engine=True):
        self.nc = nc
        self.streams = {e: [] for e in ENGS}
        self.cnt = {e: 0 for e in ENGS}
        self.waited = {e: {} for e in ENGS}
        self.last_w = {}
        self.readers = {}
        self.dma_cnt = {}
        self.sync_same = sync_same_engine

    def op(self, eng, fn, reads=(), writes=(), dma=None, cc=None):
        deps = []
        for k in reads:
            t = self.last_w.get(k)
            if t is not None:
                deps.append(t)
        for k in writes:
            t = self.last_w.get(k)
            if t is not None:
                deps.append(t)
            deps.extend(self.readers.get(k, ()))
        if cc is not None:
            key = ("cc", cc)
            self.dma_cnt[key] = self.dma_cnt.get(key, 0) + 1
            tok = (key, self.dma_cnt[key])
        elif dma is not None:
            key = ("dma", dma)
            self.dma_cnt[key] = self.dma_cnt.get(key, 0) + 16
            tok = (key, self.dma_cnt[key])
        else:
            self.cnt[eng] += 1
            tok = (eng, self.cnt[eng])
        need = {}
        for (sk, v) in deps:
            if sk == eng and (eng == "pe" or not self.sync_same):
                continue
            if v > need.get(sk, 0):
                need[sk] = v
        waits = []
        wd = self.waited[eng]
        for sk, v in need.items():
            if wd.get(sk, 0) >= v:
                continue
            wd[sk] = v
            waits.append((sk, v))
        self.streams[eng].append((waits, fn, tok))
        for k in reads:
            self.readers.setdefault(k, []).append(tok)
        for k in writes:
            self.last_w[k] = tok
            self.readers[k] = []
        return tok

    def wait_all(self, eng, toks):
        need = {}
        for (sk, v) in toks:
            if v > need.get(sk, 0):
                need[sk] = v
        self.streams[eng].append((list(need.items()), None, None))

    def barrier(self):
        toks = []
        for e in ENGS[:4]:
            if self.cnt[e]:
                toks.append((e, self.cnt[e]))
        for k, v in self.dma_cnt.items():
            toks.append((k, v))
        for e in ENGS:
            need = {}
            wd = self.waited[e]
            for sk, v in toks:
                if sk == e or wd.get(sk, 0) >= v:
                    continue
                wd[sk] = v
                need[sk] = v
            self.streams[e].append((list(need.items()), None, None))

    def emit(self, st):
        nc = self.nc
        if not hasattr(self, "sems"):
            self.sems = {}
        sems = self.sems
        keys = list(ENGS[:4]) + list(self.dma_cnt.keys())
        for k in keys:
            if k not in sems:
                sems[k] = st.enter_context(nc.semaphore(f"s{len(sems)}"))
        streams = self.streams
        self.streams = {e: [] for e in ENGS}

        def replay(stream, e):
            for waits, fn, tok in stream:
                for sk, v in waits:
                    e.wait_ge(sems[sk], v)
                if fn is None:
                    continue
                inst = fn(e)
                if isinstance(tok[0], tuple):
                    if tok[0][0] == "cc":
                        inst.then_inc(sems[tok[0]])
                    else:
                        inst.then_inc(sems[tok[0]], 16)
                else:
                    inst.then_inc(sems[tok[0]], 1)

        with nc.Block() as block:
            @block.tensor
            def _(e):
                replay(streams["pe"], e)

            @block.scalar
            def _(e):
                replay(streams["act"], e)

            @block.vector
            def _(e):
                replay(streams["dve"], e)

            @block.gpsimd
            def _(e):
                replay(streams["pool"], e)

            @block.sync
            def _(e):
                replay(streams["sp"], e)


class Rot:
    def __init__(self, name, bufs):
        self.name = name
        self.bufs = bufs
        self.i = 0

    def get(self):
        i = self.i % len(self.bufs)
        self.i += 1
        return self.bufs[i], f"{self.name}{i}"


def build(stage=99, dev=False):
    nc = bass.Bass("TRN2", target_bir_lowering=False)
    din = lambda n, s: nc.dram_tensor(n, s, F32, kind="ExternalInput").ap()
    dout = lambda n, s: nc.dram_tensor(n, s, F32, kind="ExternalOutput").ap()
    xa = din("xa", [TT, D])
    xb = din("xb", [TH, D])
    win = din("win", [D, NCOL])
    g_mix = din("g_mix", [128, D])
    g_ffn = din("g_ffn", [128, D])
    g_fin = din("g_fin", [128, D])
    pvec_d = din("pvec", [128, 24])
    wlo_d = din("wlo", [128, 256])
    wgu_d = din("wgu", [160, 256])
    ck_d = din("ck", [2, 2048, 256])
    cv_d = din("cv", [2, 2048, 256])
    swkv_d = din("swkv", [2, 4, 64, 64])
    sshift_d = din("sshift", [128, 2, 9])
    wout_d = din("wout", [D, D])
    wg_d = din("wg", [D, FF])
    wu_d = din("wu", [D, FF])
    wd_d = din("wd", [FF, D])
    rsel_d = din("rsel", [128, 2])

    ko = dout("ko", [TT, 256])
    vo = dout("vo", [TT, 256])
    sho = dout("sho", [128, 3, 9])
    wkvo = dout("wkvo", [3, 4, 64, 64])
    yo = dout("yo", [TH, D])
    if dev:
        dbg_mix = nc.dram_tensor("dbg_mix", [4 * 128, TT], BF16, kind="ExternalOutput").ap()

    mixb_in = nc.dram_tensor("mixb_in", [4 * 128, TT], BF16)
    mixb_outs = [nc.dram_tensor(f"mixb_out{c}", [2 * 128, TT], BF16) for c in range(4)]

    P = Prog(nc)
    OUTQ = os.environ.get("K_OUTQ", "sp")
    sink = [None]

    class PH:
        tok = None

    def A(eng, fn, r=(), w=(), **kw):
        if sink[0] is None:
            return P.op(eng, fn, reads=r, writes=w, **kw)
        ph = PH()
        sink[0].append((eng, fn, r, w, kw, ph))
        return ph

    PSUM_KEYS = {"mm0", "mm1", "mm2", "tpb", "X0", "X1", "oT0", "oT1"}

    def safe_boundaries(items):
        n = len(items)
        unsafe = [False] * n
        st_ = {}
        spans = []
        for i, (eng, fn, r, w, kw, ph) in enumerate(items):
            for k in r:
                if k in PSUM_KEYS and k in st_:
                    st_[k][1] = i
                    st_[k][2] = True
            for k in w:
                if k in PSUM_KEYS:
                    if k in st_ and st_[k][2]:
                        spans.append((st_[k][0], st_[k][1]))
                        st_[k] = [i, i, False]
                    elif k not in st_:
                        st_[k] = [i, i, False]
                    else:
                        st_[k][1] = i
        for k, v in st_.items():
            spans.append((v[0], v[1]))
        for a, b in spans:
            for i in range(a, b):
                unsafe[i] = True
        return unsafe

    def replay_items(items, every, mode=None):
        unsafe = safe_boundaries(items)
        since = 0
        nev = 0
        kmin = int(os.environ.get("K_EVMIN", "2"))
        for i, (eng, fn, r, w, kw, ph) in enumerate(items):
            ph.tok = P.op(eng, fn, reads=r, writes=w, **kw)
            since += 1
            if mode == "evac":
                if eng != "pe" and any(k in PSUM_KEYS for k in r):
                    nev += 1
                if nev >= kmin and not unsafe[i]:
                    nev = 0
                    since = 0
                    yield
                elif since >= 4 * every and not unsafe[i]:
                    since = 0
                    yield
            elif since >= every and not unsafe[i]:
                since = 0
                yield

    def interleave(entries):
        act = [[g, max(1, n), 0] for g, n in entries]
        while act:
            g = min(act, key=lambda x: x[2] / x[1])
            try:
                next(g[0])
                g[2] += 1
            except StopIteration:
                act.remove(g)

    out_toks = []

    with contextlib.ExitStack() as st:
        stA = contextlib.ExitStack()
        cur = {"st": st}

        def sb(name, shape, dt=F32):
            return cur["st"].enter_context(nc.sbuf_tensor(name, shape, dt))

        def ps(name, shape, dt=F32):
            return st.enter_context(nc.psum_tensor(name, shape, dt))

        mm = Rot("mm", [ps(f"mm{i}", [128, 512]) for i in range(3)])
        tpb = ps("tpb", [128, 1024], BF16)
        Xb = Rot("X", [ps(f"X{i}", [128, 512]) for i in range(2)])
        oTb = Rot("oT", [ps(f"oT{i}", [128, 512]) for i in range(2)])
        zbk = Rot("X", Xb.bufs)
        aX = Rot("oT", [oTb.bufs[0]])
        aO = Rot("oT1_", [oTb.bufs[1]])

        ident_bf = sb("ident_bf", [128, 128], BF16)
        rsel = sb("rsel_sb", [128, 2])
        cur["st"] = stA
        ones_bf = sb("ones_bf", [128, 512], BF16)
        ident_f = sb("ident_f", [128, 128])
        tri_bf = sb("tri_bf", [128, 128], BF16)
        lst_bf = sb("lst_bf", [128, 128], BF16)
        dmask = sb("dmask", [128, 896], BF16)
        smask_dg = sb("smask_dg", [128, 4, 32], BF16)
        blk1 = sb("blk1", [128, 128])
        blk64 = sb("blk64", [128, 128])
        m_su = sb("m_su", [128, 2, 128], BF16)
        m_sui = sb("m_sui", [128, 2, 128], BF16)
        m_sl = sb("m_sl", [128, 2, 128], BF16)
        scm = sb("scm", [128, 512], BF16)
        scm_s = sb("scm_s", [128, 64])
        pvec = sb("pvec_sb", [128, 24])
        gmix = sb("gmix", [128, D])

        A("pool", lambda e: e.memset(ones_bf[:], 1.0), w=["ones_bf"])

        def asel(out, in_, pat, op, base, cm, rk, wk):
            A("pool", lambda e: e.affine_select(out=out, in_=in_, pattern=pat, compare_op=op,
                                               fill=0.0, base=base, channel_multiplier=cm), r=rk, w=wk)

        asel(ident_bf[:], ones_bf[:, 0:128], [[-1, 128]], ALU.is_equal, 0, 1, ["ones_bf"], ["ident_bf"])
        A("pool", lambda e: e.tensor_copy(out=ident_f[:], in_=ident_bf[:]), r=["ident_bf"], w=["ident_f"])
        asel(tri_bf[:], ones_bf[:, 0:128], [[-1, 128]], ALU.is_ge, 0, 1, ["ones_bf"], ["tri_bf"])
        asel(lst_bf[:], ones_bf[:, 0:128], [[1, 128]], ALU.is_gt, 0, -1, ["ones_bf"], ["lst_bf"])
        asel(dmask[:, 0:512], ones_bf[:], [[1, 512]], ALU.is_gt, -384, -1, ["ones_bf"], ["dmask"])
        asel(dmask[:, 512:896], ones_bf[:, 0:384], [[1, 384]], ALU.is_gt, 128, -1, ["ones_bf"], ["dmask"])
        asel(smask_dg[:], ones_bf[:, 0:128].rearrange("p (h t) -> p h t", t=32), [[0, 4], [1, 32]],
             ALU.is_gt, 0, -1, ["ones_bf"], ["smask_dg"])
        for hh in range(2):
            o2 = ones_bf[:, 0:128]
            asel(m_su[:, hh, :], o2, [[1, 128]], ALU.is_gt, 0, -1, ["ones_bf"], ["m_su"])
            asel(m_sui[:, hh, :], o2, [[1, 128]], ALU.is_ge, 0, -1, ["ones_bf"], ["m_sui"])
            asel(m_sl[:, hh, :], o2, [[-1, 128]], ALU.is_gt, 0, 1, ["ones_bf"], ["m_sl"])
        A("pool", lambda e: e.memset(blk1[:], 0.0), w=["blk1"])
        A("pool", lambda e: e.memset(blk64[:], 0.0), w=["blk64"])
        for hh in range(2):
            sl = slice(hh * 64, hh * 64 + 64)
            A("pool", lambda e, sl=sl: e.memset(blk1[sl, sl], 1.0), w=["blk1"])
            A("pool", lambda e, sl=sl: e.memset(blk64[sl, sl], 1.0 / 64), w=["blk64"])
        A("pool", lambda e: e.memset(scm[:], 1.0), w=["scm"])
        A("pool", lambda e: e.memset(scm[:].rearrange("p (c t) -> p c t", t=128)[:, :, 0:1], 0.0), w=["scm"])
        A("pool", lambda e: e.memset(scm_s[:], 1.0), w=["scm_s"])
        A("pool", lambda e: e.memset(scm_s[:].rearrange("p (c t) -> p c t", t=32)[:, :, 0:1], 0.0), w=["scm_s"])
        A("sp", lambda e: e.dma_start(out=pvec[:], in_=pvec_d), w=["pvec"], dma="c0a")
        A("sp", lambda e: e.dma_start(out=gmix[:], in_=g_mix), w=["gmix"], dma="c0b")
        A("sp", lambda e: e.dma_start(out=rsel[:], in_=rsel_d), w=["rsel"], dma="c0c")

        win_bf = sb("win_bf", [128, 8, NCOL], BF16)
        XPp = sb("XPp", [128, 9, 1, 513])
        xt = Rot("xt", [sb(f"xt{i}", [128, D]) for i in range(2)])
        for kc in range(8):
            A("pool", lambda e, kc=kc: e.dma_start(out=win_bf[:, kc, :], in_=win[kc * 128:(kc + 1) * 128, :]),
              w=["win_bf"], dma="win_bf")
        A("pool", lambda e: e.memset(XPp[:].rearrange("p a b c -> p (a b c)"), 0.0), r=[], w=["XP"])
        wlo_bf = sb("wlo_bf", [128, 256], BF16)
        wgu_bf = sb("wgu_bf", [128, 2, 256], BF16)
        xtt, xk = xt.get()
        A("sp", lambda e, xtt=xtt: e.dma_start(out=xtt[:, 0:256], in_=wlo_d), w=[xk], dma=xk)
        A("sp", lambda e, xtt=xtt: e.dma_start(out=xtt[:, 256:512], in_=wgu_d[0:128, :]), w=[xk], dma=xk)
        A("sp", lambda e, xtt=xtt: e.dma_start(out=xtt[0:32, 512:768], in_=wgu_d[128:160, :]), w=[xk], dma=xk)
        A("dve", lambda e, xtt=xtt: e.tensor_copy(out=wlo_bf[:], in_=xtt[:, 0:256]), r=[xk], w=["wlo_bf"])
        A("dve", lambda e, xtt=xtt: e.tensor_copy(out=wgu_bf[:, 0, :], in_=xtt[:, 256:512]), r=[xk], w=["wgu_bf"])
        A("dve", lambda e, xtt=xtt: e.tensor_copy(out=wgu_bf[0:32, 1, :], in_=xtt[0:32, 512:768]), r=[xk], w=["wgu_bf"])

        stat = Rot("stat", [sb(f"stat{i}", [128, 4]) for i in range(2)])
        hbf = Rot("hbf", [sb(f"hbf{i}", [128, D], BF16) for i in range(2)])
        hT = Rot("hT", [sb(f"hT{i}", [128, 8, 512], BF16) for i in range(1)])
        qT = Rot("qT", [sb(f"qT{i}", [128, 2, 512], BF16) for i in range(2)])
        kT = sb("kT", [128, 2, T], BF16)
        kTn = sb("kTn", [128, 2, 2, 128], BF16)
        vpad = sb("vpad", [128, 32, 4, 128], BF16)
        vnpad = sb("vnpad", [128, 2, 4, 128], BF16)
        kvs = Rot("kvs", [sb(f"kvs{i}", [128, 512]) for i in range(1)])
        XPs = sb("XPs", [128, 9, 2, 33])
        e_t = Rot("e_t", [sb(f"e_t{i}", [128, 512], BF16) for i in range(3)])
        sp_t = Rot("sp_t", [sb(f"sp_t{i}", [128, 512], BF16) for i in range(3)])
        ex_t = Rot("ex_t", [sb(f"ex_t{i}", [128, 512], BF16) for i in range(2)])
        w_t = Rot("w_t", [sb(f"w_t{i}", [128, 512], BF16) for i in range(2)])
        osb = Rot("osb", [sb("osb0", [128, 512], BF16)])
        orw = Rot("orw", [sb("orw0", [128, 512], BF16)])

        A("pool", lambda e: e.memset(vpad[:].rearrange("p a b c -> p (a b c)"), 0.0), w=[f"vpad{i}" for i in range(8)])
        A("pool", lambda e: e.memset(vnpad[:].rearrange("p s b c -> p (s b c)"), 0.0), w=["vnpad"])
        A("pool", lambda e: e.memset(kTn[:].rearrange("p s b c -> p (s b c)"), 0.0), w=["kTn"])
        A("pool", lambda e: e.memset(XPs[:].rearrange("p a b c -> p (a b c)"), 0.0), w=["XPs"])

        c_tiles = [(0, 128), (128, 128), (256, 128), (384, 128)]
        r_tiles = [(768 + 128 * i, 128) for i in range(6)] + [(1536, 128), (1664, 128), (1792, 32)]

        def transpose_tile(hb, hbk, nrows, hTt, hTk, col0, evac_eng):
            for kc in range(8):
                A("pe", lambda e, kc=kc: e.transpose(out=tpb[:, kc * 128:kc * 128 + nrows],
                                                     in_=hb[:nrows, kc * 128:(kc + 1) * 128],
                                                     identity=ident_bf[:nrows, :nrows]),
                  r=[hbk, "ident_bf"], w=["tpb"])
            src = tpb[:].rearrange("p (k t) -> p k t", t=128)[:, :, 0:nrows]
            dst = hTt[:, :, col0:col0 + nrows]
            if evac_eng == "act":
                A("act", lambda e: e.activation(out=dst, in_=src, func=AF.Copy), r=["tpb"], w=[hTk])
            else:
                A("dve", lambda e: e.tensor_copy(out=dst, in_=src), r=["tpb"], w=[hTk])

        def project_block(blk):
            hTt, hTk = hT.get()
            ntok = blk["ntok"]
            for ti, (row0, nrows) in enumerate(blk["tiles"]):
                xtt, xk = xt.get()
                A("sp", lambda e, xtt=xtt, row0=row0, nrows=nrows: e.dma_start(
                    out=xtt[:nrows, :], in_=xa[row0:row0 + nrows, :]), w=[xk], dma=xk)
                hb, hbk = hbf.get()
                stt_r = [xk]
                stt, sk = stat.get()
                A("act", lambda e, xtt=xtt, stt=stt, nrows=nrows, hb=hb: e.activation(
                    out=hb[:nrows, :], in_=xtt[:nrows, :], func=AF.Square, accum_out=stt[:nrows, 0:1]),
                  r=[xk], w=[hbk, sk])
                A("act", lambda e, stt=stt, nrows=nrows: e.activation(
                    out=stt[:nrows, 1:2], in_=stt[:nrows, 0:1], func=AF.Ln, scale=1.0 / D, bias=1e-6),
                  r=[sk], w=[sk])
                A("act", lambda e, stt=stt, nrows=nrows: e.activation(
                    out=stt[:nrows, 2:3], in_=stt[:nrows, 1:2], func=AF.Exp, scale=-0.5), r=[sk], w=[sk])
                A("dve", lambda e, xtt=xtt, stt=stt, hb=hb, nrows=nrows: e.scalar_tensor_tensor(
                    out=hb[:nrows, :], in0=xtt[:nrows, :], scalar=stt[:nrows, 2:3], in1=gmix[:nrows, :],
                    op0=ALU.mult, op1=ALU.mult), r=[xk, sk, "gmix"], w=[hbk])
                transpose_tile(hb, hbk, nrows, hTt, hTk, row0 - blk["tok0"], "act" if ti % 2 == 0 else "dve")
            return hTt, hTk

        def proj_feature_major(blk, hTt, hTk, qTt, qTk, which="all"):
            ntok = blk["ntok"]
            sample = blk["sample"]
            for ci, (c0, wdt) in enumerate(c_tiles if which in ("all", "qk") else []):
                bank, bk = mm.get()
                for kc in range(8):
                    A("pe", lambda e, kc=kc, c0=c0, bank=bank: e.matmul(
                        bank[:, :ntok], lhsT=win_bf[:, kc, c0:c0 + 128], rhs=hTt[:, kc, :ntok],
                        start=(kc == 0), stop=(kc == 7)), r=["win_bf", hTk], w=[bk])
                p = ci % 2
                if ci < 2:
                    A("act", lambda e, bank=bank, p=p: e.activation(out=qTt[:, p, :ntok], in_=bank[:, :ntok],
                                                                     func=AF.Copy), r=[bk], w=[qTk])
                elif not sample:
                    t0 = blk["tok0"]
                    A("dve", lambda e, bank=bank, p=p, t0=t0: e.tensor_copy(out=kT[:, p, t0:t0 + ntok],
                                                                             in_=bank[:, :ntok]),
                      r=[bk], w=[f"kT{t0 // 512}"])
                else:
                    A("dve", lambda e, bank=bank, p=p: e.tensor_copy(
                        out=kTn[:, p, :, 0:NS],
                        in_=bank[:, :ntok].rearrange("p (s t) -> p s t", t=NS)), r=[bk], w=["kTn"])
            for ci, (c0, wdt) in enumerate(r_tiles if which in ("all", "rw") else []):
                bank, bk = mm.get()
                for kc in range(8):
                    A("pe", lambda e, kc=kc, c0=c0, wdt=wdt, bank=bank: e.matmul(
                        bank[:wdt, :ntok], lhsT=win_bf[:, kc, c0:c0 + wdt], rhs=hTt[:, kc, :ntok],
                        start=(kc == 0), stop=(kc == 7)), r=["win_bf", hTk], w=[bk])
                eng = "act" if ci % 4 == 0 else "dve"
                if not sample:
                    dst = XPp[:wdt, ci, 0, 1:513]
                    src = bank[:wdt, :ntok]
                    wk = "XP"
                else:
                    dst = XPs[:wdt, ci, :, 1:33]
                    src = bank[:wdt, :ntok].rearrange("p (s t) -> p s t", t=NS)
                    wk = "XPs"
                if eng == "act":
                    A("act", lambda e, dst=dst, src=src: e.activation(out=dst, in_=src, func=AF.Copy),
                      r=[bk], w=[wk])
                else:
                    A("dve", lambda e, dst=dst, src=src: e.tensor_copy(out=dst, in_=src), r=[bk], w=[wk])

        def proj_token_major(blk, hTt, hTk):
            for ti, (row0, nrows) in enumerate(blk["tiles"]):
                col0 = row0 - blk["tok0"]
                bank, bk = mm.get()
                for kc in range(8):
                    A("pe", lambda e, kc=kc, bank=bank, col0=col0, nrows=nrows: e.matmul(
                        bank[:nrows, :], lhsT=hTt[:, kc, col0:col0 + nrows], rhs=win_bf[:, kc, 256:768],
                        start=(kc == 0), stop=(kc == 7)), r=["win_bf", hTk], w=[bk])
                kv, kvk = kvs.get()
                A("act", lambda e, kv=kv, bank=bank, nrows=nrows: e.activation(
                    out=kv[:nrows, :], in_=bank[:nrows, :], func=AF.Copy), r=[bk], w=[kvk])
                SKIP = int(os.environ.get("K_SKIP", "0"))
                if not (SKIP & 1):
                  out_toks.append(A(OUTQ, lambda e, kv=kv, row0=row0, nrows=nrows: e.dma_start(
                    out=ko[row0:row0 + nrows, :], in_=kv[:nrows, 0:256]), r=[kvk], dma="o" + kvk))
                if not (SKIP & 1):
                  out_toks.append(A(OUTQ, lambda e, kv=kv, row0=row0, nrows=nrows: e.dma_start(
                    out=vo[row0:row0 + nrows, :], in_=kv[:nrows, 256:512]), r=[kvk], dma="o" + kvk))
                src = kv[:nrows, 256:512].rearrange("p (a b d) -> p a b d", a=2, b=2)
                if not blk["sample"]:
                    kt = row0 // 128
                    dstf = vpad[:nrows, kt, :, :].rearrange("p h c -> p (h c)")
                    wk = f"vpad{row0 // 512}"
                else:
                    sq = ti
                    dstf = vnpad[:nrows, sq, :, :].rearrange("p h c -> p (h c)")
                    wk = "vnpad"
                for b_ in range(2):
                    dst = dstf[:, b_ * 192: b_ * 192 + 64 + 256].rearrange("p (a r) -> p a r", r=320) \
                        if False else None
                for a_ in range(2):
                    for b_ in range(2):
                        if SKIP & 2:
                            continue
                        off = a_ * 256 + b_ * 192
                        A("dve", lambda e, off=off, a_=a_, b_=b_, dstf=dstf, src=src: e.tensor_copy(
                            out=dstf[:, off:off + 64], in_=src[:, a_, b_, :]), r=[kvk], w=[wk])

        def attn_unit(ncols, groups, kblocks, zsingle=False):
            X, Xk = aX.bufs[0], "oT0"
            nb = len(kblocks)
            st_ = {}

            def stage_a(bi):
                kb = kblocks[bi]
                nk = kb["nk"]
                zb, zk = zbk.get()
                if zsingle:
                    zb, zk = zbk.bufs[0], "X0"
                for g in groups:
                    A("pe", lambda e, g=g, kb=kb, zb=zb, nk=nk: e.matmul(
                        zb[:nk, g["c0"]:g["c0"] + g["nc"]], lhsT=g["kT"](kb), rhs=g["q"], start=True, stop=True),
                      r=g["rk"] + ([f"kT{kb['kt'] // 4}"] if kb["kt"] >= 0 else []), w=[zk])
                et, ek = e_t.get()
                A("act", lambda e, et=et, zb=zb, nk=nk: e.activation(out=et[:nk, :ncols], in_=zb[:nk, :ncols],
                                                                      func=AF.Exp, scale=0.125), r=[zk], w=[ek])
                spt, spk = sp_t.get()
                A("act", lambda e, et=et, spt=spt, nk=nk: e.activation(out=spt[:nk, :ncols], in_=et[:nk, :ncols],
                                                                        func=AF.Ln, bias=1.0), r=[ek], w=[spk])
                if kb["mask"] is not None:
                    mk = kb["mask"]
                    A("dve", lambda e, spt=spt, mk=mk, nk=nk: e.tensor_tensor(
                        out=spt[:nk, :ncols], in0=spt[:nk, :ncols], in1=mk, op=ALU.mult), r=[spk, "dmask"], w=[spk])
                    A("pool", lambda e, et=et, mk=mk, nk=nk: e.tensor_tensor(
                        out=et[:nk, :ncols], in0=et[:nk, :ncols], in1=mk, op=ALU.mult), r=[ek, "dmask"], w=[ek])
                st_[bi] = (et, ek, spt, spk)

            def pv(bi):
                kb = kblocks[bi]
                nk = kb["nk"]
                wt, wk = st_[("w", bi)]
                for g in groups:
                    A("pe", lambda e, g=g, kb=kb, wt=wt, nk=nk, bi=bi: e.matmul(
                        g["out"], lhsT=g["v"](kb), rhs=wt[:nk, g["c0"]:g["c0"] + g["nc"]],
                        start=(g["first"] and bi == 0), stop=(g["last"] and bi == nb - 1),
                        skip_group_check=True),
                      r=[wk] + g["rv"] + ([f"vpad{kb['kt'] // 4}"] if kb["kt"] >= 0 else []), w=[g["ok"]])

            stage_a(0)
            if nb > 1:
                stage_a(1)
            for bi, kb in enumerate(kblocks):
                nk = kb["nk"]
                et, ek, spt, spk = st_.pop(bi)
                A("pe", lambda e, X=X, spt=spt, nk=nk, bi=bi: e.matmul(
                    X[:, :ncols], lhsT=tri_bf[:nk, :], rhs=spt[:nk, :ncols], start=(bi == 0), stop=False,
                    skip_group_check=True), r=[spk, "tri_bf"], w=[Xk])
                ext, exk = ex_t.get()
                A("act", lambda e, ext=ext, X=X, nk=nk: e.activation(out=ext[:nk, :ncols], in_=X[:nk, :ncols],
                                                                      func=AF.Exp, scale=-1.0), r=[Xk], w=[exk])
                if bi + 2 < nb:
                    stage_a(bi + 2)
                if bi > 0:
                    pv(bi - 1)
                if bi < nb - 1:
                    A("pe", lambda e, X=X, spt=spt, nk=nk, bi=bi: e.matmul(
                        X[:, :ncols], lhsT=lst_bf[:nk, :], rhs=spt[:nk, :ncols], start=False, stop=(bi == nb - 2),
                        skip_group_check=True), r=[spk, "lst_bf"], w=[Xk])
                wt, wk = w_t.get()
                A("dve", lambda e, wt=wt, et=et, ext=ext, nk=nk: e.tensor_tensor(
                    out=wt[:nk, :ncols], in0=et[:nk, :ncols], in1=ext[:nk, :ncols], op=ALU.mult),
                  r=[ek, exk], w=[wk])
                st_[("w", bi)] = (wt, wk)
                yield
            pv(nb - 1)

        def mix_store(src_ap, srck, chunk, col0, ncol):
            return A(OUTQ, lambda e: e.dma_start(out=mixb_in[chunk * 128:(chunk + 1) * 128, col0:col0 + ncol],
                                                   in_=src_ap), r=[srck], w=["mixb_in"], dma="m" + srck)

        def attention_prompt(qb, qTt, qTk):
            for p in range(2):
                oT, oTk = aO.bufs[0], "oT1"
                for e_ in range(2):
                    h = 2 * p + e_
                    sl = slice(e_ * 64, e_ * 64 + 64)
                    kbs = []
                    for kt in range(4 * qb + 3, -1, -1):
                        i = kt - 4 * qb
                        kbs.append(dict(nk=128, kt=kt, mask=(dmask[:, 384 - 128 * i:896 - 128 * i] if i >= 0 else None)))
                    g = dict(c0=0, nc=512, q=qTt[sl, p, :], rk=[qTk], rv=[],
                             kT=lambda kb, sl=sl, p=p: kT[sl, p, kb["kt"] * 128:(kb["kt"] + 1) * 128],
                             v=lambda kb, h=h: vpad[:, kb["kt"], h, :],
                             out=oT[:, :], ok=oTk, first=(e_ == 0), last=(e_ == 1))
                    yield from attn_unit(512, [g], kbs)
                ob, obk = osb.get()
                A("act", lambda e, ob=ob, oT=oT: e.activation(out=ob[:, :], in_=oT[:, :], func=AF.Copy),
                  r=[oTk], w=[obk])
                mix_store(ob[:, :], obk, p, qb * 512, 512)

        def attention_sample(qTt, qTk):
            SAMP = int(os.environ.get("K_SAMP", "7"))
            for sq in range(2):
                if SAMP & 1:
                    load_cache(sq)
                if not (SAMP & 6):
                    continue
                oTs = [(aO.bufs[0], "oT1"), (zbk.bufs[1], "X1")]
                kbs = [dict(nk=128, kt=-1, mask=smask_dg[:].rearrange("p h t -> p (h t)"))]
                for kt in range(31, 15, -1):
                    if SAMP & 4:
                        kbs.append(dict(nk=128, kt=kt, mask=None))
                msk = smask_dg[:, 0:2, :].rearrange("p h t -> p (h t)")
                for kb in kbs:
                    if kb["mask"] is not None:
                        kb["mask"] = msk
                for e_ in range(2):
                    sl = slice(e_ * 64, e_ * 64 + 64)
                    groups = []
                    for p in range(2):
                        h = 2 * p + e_
                        groups.append(dict(
                            c0=p * NS, nc=NS, q=qTt[sl, p, sq * NS:(sq + 1) * NS], rk=[qTk, "kTn"],
                            rv=["vnpad"],
                            kT=lambda kb, sl=sl, p=p, sq=sq: (kTn[sl, p, sq, :] if kb["kt"] < 0 else
                                                              kT[sl, p, kb["kt"] * 128:(kb["kt"] + 1) * 128]),
                            v=lambda kb, h=h, sq=sq: (vnpad[:, sq, h, :] if kb["kt"] < 0 else vpad[:, kb["kt"], h, :]),
                            out=oTs[p][0][:, 0:NS], ok=oTs[p][1], first=(e_ == 0), last=(e_ == 1)))
                    yield from attn_unit(2 * NS, groups, kbs, zsingle=True)
                for p in range(2):
                    ob, obk = osb.get()
                    A("act", lambda e, ob=ob, p=p: e.activation(out=ob[:, 0:NS], in_=oTs[p][0][:, 0:NS],
                                                                 func=AF.Copy), r=[oTs[p][1]], w=[obk])
                    mix_store(ob[:, 0:NS], obk, p, T + sq * NS, NS)

        def load_cache(sq):
            for kt in range(16, 32):
                xtt, xk = xt.get()
                A("sp", lambda e, xtt=xtt, kt=kt: e.dma_start(
                    out=xtt[:, 0:256], in_=ck_d[sq, (kt - 16) * 128:(kt - 15) * 128, :]), w=[xk], dma=xk)
                A("sp", lambda e, xtt=xtt, kt=kt: e.dma_start(
                    out=xtt[:, 256:512], in_=cv_d[sq, (kt - 16) * 128:(kt - 15) * 128, :]), w=[xk], dma=xk)
                hb, hbk = hbf.get()
                A("dve", lambda e, hb=hb, xtt=xtt: e.tensor_copy(out=hb[:, 0:256], in_=xtt[:, 0:256]),
                  r=[xk], w=[hbk])
                for p in range(2):
                    A("pe", lambda e, hb=hb, p=p: e.transpose(out=tpb[:, p * 128:(p + 1) * 128],
                                                              in_=hb[:, p * 128:(p + 1) * 128],
                                                              identity=ident_bf[:]),
                      r=[hbk, "ident_bf"], w=["tpb"])
                A("act", lambda e, kt=kt: e.activation(
                    out=kT[:, :, kt * 128:(kt + 1) * 128],
                    in_=tpb[:, 0:256].rearrange("p (a t) -> p a t", t=128), func=AF.Copy),
                  r=["tpb"], w=[f"kT{kt // 4}"])
                dstf = vpad[:, kt, :, :].rearrange("p h c -> p (h c)")
                for a_ in range(2):
                    for b_ in range(2):
                        off = a_ * 256 + b_ * 192
                        hidx = 2 * a_ + b_
                        A("pool", lambda e, off=off, hidx=hidx, dstf=dstf, xtt=xtt: e.tensor_copy(
                            out=dstf[:, off:off + 64], in_=xtt[:, 256 + hidx * 64:256 + hidx * 64 + 64]),
                          r=[xk], w=[f"vpad{kt // 4}"])


        SH = Rot("SH", [sb(f"SH{i}", [128, 512]) for i in range(1)])
        lo_bf = sb("lo_bf", [128, 512], BF16)
        sg_bf = sb("sg_bf", [128, 2, 512], BF16)
        SWt = sb("SWt", [128, 2, 512])
        At = sb("At", [128, 2, 512])
        Gt = sb("Gt", [128, 2, 512], BF16)
        Fs = [sb(f"F{i}", [128, 512]) for i in range(8)]
        Hs = [sb(f"H{i}", [128, 512], BF16) for i in range(7)]
        VZ = sb("VZ", [128, 4, 192], BF16)
        BKt = sb("BKt", [128, 4, 256], BF16)
        M4 = [sb(f"M4_{e_}", [128, 4, 128], BF16) for e_ in range(2)]
        RKm = sb("RKm", [128, 2, 128], BF16)
        XX = [sb(f"XX{i}", [128, 4, 128], BF16) for i in range(2)]
        PP = [sb(f"PP{i}", [128, 2, 128], BF16) for i in range(2)]
        mk4 = sb("mk4", [128, 4, 128], BF16)
        S_f = [sb(f"S_f{p}", [128, 128]) for p in range(4)]
        S_bf = [sb(f"S_bf{p}", [128, 128], BF16) for p in range(4)]
        W1b = sb("W1b", [128, 128], BF16)
        UZ = sb("UZ", [128, 192], BF16)
        gct = sb("gct", [128, 8])
        pv2 = sb("pv2", [128, 2])
        sst = sb("sst", [128, 2, 9])
        SW0 = sb("SW0", [64, 2, 64])
        Sout = sb("Sout", [128, 64])
        id2 = sb("id2", [128, 2, 128], BF16)

        A("pool", lambda e: e.memset(VZ[:].rearrange("p a b -> p (a b)"), 0.0), w=["VZ"])
        A("pool", lambda e: e.memset(UZ[:], 0.0), w=["UZ"])
        for i_, mk_ in enumerate([m_su, m_sl, m_su, m_sui]):
            A("pool", lambda e, i_=i_, mk_=mk_: e.tensor_copy(out=mk4[:, i_, :], in_=mk_[:, 0, :]),
              r=["m_su", "m_sl", "m_sui"], w=["mk4"])
        for e_ in range(2):
            A("pool", lambda e, e_=e_: e.tensor_copy(out=id2[:, e_, :], in_=ident_bf[:]), r=["ident_bf"], w=["id2"])
        for p in range(2):
            A("dve", lambda e, p=p: e.tensor_scalar(out=pv2[:, p:p + 1], in0=pvec[:, 9 + p * 7 + 3:9 + p * 7 + 4],
                                                    scalar1=-1.0, scalar2=1.0, op0=ALU.mult, op1=ALU.add),
              r=["pvec"], w=["pv2"])
        A("sp", lambda e: e.dma_start(out=sst[:], in_=sshift_d), w=["sst"], dma="c2a")
        for sq in range(2):
            A("pool", lambda e, sq=sq: e.tensor_copy(out=XPs[:, :, sq, 0], in_=sst[:, sq, :]), r=["sst"], w=["XPs"])

        def state_zero(si):
            A("pool", lambda e: e.memset(S_f[si][:], 0.0), w=[f"S_f{si}"])
            A("pool", lambda e: e.memset(S_bf[si][:], 0.0), w=[f"S_bf{si}"])

        def state_load(si, p, sq):
            state_zero(si)
            A("sp", lambda e: e.dma_start(out=SW0[:], in_=swkv_d[sq, 2 * p:2 * p + 2].rearrange("e v k -> v e k")),
              w=["SW0"], dma="sw0")
            bank, bk = mm.get()
            A("pe", lambda e: e.transpose(out=bank[:, 0:64], in_=SW0[:].rearrange("v e k -> v (e k)"),
                                          identity=ident_f[0:64, 0:64]), r=["SW0", "ident_f"], w=[bk])
            for e_ in range(2):
                sl = slice(e_ * 64, e_ * 64 + 64)
                A("dve", lambda e, sl=sl: e.tensor_copy(out=S_f[si][sl, sl], in_=bank[sl, 0:64]), r=[bk], w=[f"S_f{si}"])
            for e_ in range(2):
                sl = slice(e_ * 64, e_ * 64 + 64)
                A("pool", lambda e, sl=sl: e.tensor_copy(out=S_bf[si][sl, sl], in_=S_f[si][sl, sl]),
                  r=[f"S_f{si}"], w=[f"S_bf{si}"])

        def state_store(si, p, which):
            bank, bk = mm.get()
            A("pe", lambda e: e.transpose(out=bank[:, 0:128], in_=S_f[si][:], identity=ident_f[:]),
              r=[f"S_f{si}", "ident_f"], w=[bk])
            for e_ in range(2):
                sl = slice(e_ * 64, e_ * 64 + 64)
                A("dve", lambda e, sl=sl: e.tensor_copy(out=Sout[sl, :], in_=bank[sl, sl]), r=[bk], w=["Sout"])
            out_toks.append(A(OUTQ, lambda e: e.dma_start(
                out=wkvo[which, 2 * p:2 * p + 2].rearrange("e v k -> (e v) k"), in_=Sout[:]),
                r=["Sout"], dma="osout"))

        def rwkv_block(blk):
            sample = blk["sample"]
            if sample:
                XP, XPk, nseg, L, C = XPs, "XPs", 2, NS, NS
                scmask = scm_s
            else:
                XP, XPk, nseg, L, C = XPp, "XP", 1, 512, 128
                scmask = scm
            ntok = nseg * L
            nch = ntok // C
            nlev = int(round(math.log2(C)))
            cur = lambda ct, rows=128: XP[:rows, ct, :, 1:L + 1]
            prv = lambda ct, rows=128: XP[:rows, ct, :, 0:L]
            v3 = lambda ap, rows=128: ap[:rows, :ntok].rearrange("p (s l) -> p s l", l=L)

            for ct in range(9):
                rows = 32 if ct == 8 else 128
                sh, shk = SH.get()
                A("dve", lambda e, ct=ct, rows=rows, sh=sh: e.tensor_tensor(
                    out=v3(sh, rows), in0=prv(ct, rows), in1=cur(ct, rows), op=ALU.subtract), r=[XPk], w=[shk])
                A("pool", lambda e, ct=ct, rows=rows: e.tensor_copy(out=XP[:rows, ct, :, 0:1],
                                                                     in_=XP[:rows, ct, :, L:L + 1]),
                  r=[XPk, shk], w=[XPk])
                A("dve", lambda e, ct=ct, rows=rows, sh=sh: e.scalar_tensor_tensor(
                    out=cur(ct, rows), in0=v3(sh, rows), scalar=pvec[:rows, ct:ct + 1], in1=cur(ct, rows),
                    op0=ALU.mult, op1=ALU.add), r=[shk, XPk, "pvec"], w=[XPk])

            A("act", lambda e: e.activation(out=v3(lo_bf, 64), in_=cur(6, 64), func=AF.Tanh), r=[XPk], w=["lo_bf"])
            A("dve", lambda e: e.tensor_copy(out=v3(lo_bf)[64:128], in_=cur(6)[64:128]), r=[XPk], w=["lo_bf"])
            A("act", lambda e: e.activation(out=v3(sg_bf[:, 0, :]), in_=cur(7), func=AF.Sigmoid), r=[XPk], w=["sg_bf"])
            A("act", lambda e: e.activation(out=v3(sg_bf[:, 1, :], 32), in_=cur(8, 32), func=AF.Sigmoid),
              r=[XPk], w=["sg_bf"])
            for p in range(2):
                pc = lambda i, p=p: pvec[:, 9 + p * 7 + i:9 + p * 7 + i + 1]
                cs = slice(p * 128, (p + 1) * 128)
                bank, bk = mm.get()
                A("pe", lambda e, bank=bank, cs=cs: e.matmul(bank[:, :ntok], lhsT=wlo_bf[0:64, cs], rhs=lo_bf[0:64, :ntok],
                                                             start=True, stop=True), r=["wlo_bf", "lo_bf"], w=[bk])
                A("act", lambda e, bank=bank, p=p, pc=pc: e.activation(out=SWt[:, p, :ntok], in_=bank[:, :ntok],
                                                                      func=AF.Sigmoid, bias=pc(0)), r=[bk, "pvec"], w=["SWt"])
                bank, bk = mm.get()
                A("pe", lambda e, bank=bank, cs=cs: e.matmul(bank[:, :ntok], lhsT=wlo_bf[64:128, cs], rhs=lo_bf[64:128, :ntok],
                                                             start=True, stop=True), r=["wlo_bf", "lo_bf"], w=[bk])
                A("act", lambda e, bank=bank, p=p, pc=pc: e.activation(out=At[:, p, :ntok], in_=bank[:, :ntok],
                                                                      func=AF.Sigmoid, bias=pc(1)), r=[bk, "pvec"], w=["At"])
                bank, bk = mm.get()
                A("pe", lambda e, bank=bank, cs=cs: e.matmul(bank[:, :ntok], lhsT=wgu_bf[:, 0, cs], rhs=sg_bf[:, 0, :ntok],
                                                             start=True, stop=False), r=["wgu_bf", "sg_bf"], w=[bk])
                A("pe", lambda e, bank=bank, cs=cs: e.matmul(bank[:, :ntok], lhsT=wgu_bf[0:32, 1, cs], rhs=sg_bf[0:32, 1, :ntok],
                                                             start=False, stop=True), r=["wgu_bf", "sg_bf"], w=[bk])
                A("dve", lambda e, bank=bank, p=p: e.tensor_copy(out=Gt[:, p, :ntok], in_=bank[:, :ntok]), r=[bk], w=["Gt"])

            for p in range(2):
                rwkv_pair(blk, p, XP, XPk, nseg, L, C, ntok, nch, nlev, scmask, cur, v3)

        def rwkv_pair(blk, p, XP, XPk, nseg, L, C, ntok, nch, nlev, scmask, cur, v3):
            sample = blk["sample"]
            pc = lambda i: pvec[:, 9 + p * 7 + i:9 + p * 7 + i + 1]
            R, KR, VR = cur(p), cur(2 + p), cur(4 + p)
            F = lambda i: Fs[i][:, :ntok]
            Hh = lambda i: Hs[i][:, :ntok]
            Fk = lambda i: f"F{i}"
            Hk = lambda i: f"H{i}"
            SWp, Ap = SWt[:, p, :ntok], At[:, p, :ntok]
            A("dve", lambda e: e.tensor_scalar(out=v3(Fs[0]), in0=KR, scalar1=pc(2), scalar2=None, op0=ALU.mult),
              r=[XPk, "pvec"], w=[Fk(0)])
            A("dve", lambda e: e.tensor_tensor(out=F(1), in0=F(0), in1=F(0), op=ALU.mult), r=[Fk(0)], w=[Fk(1)])
            bank, bk = mm.get()
            A("pe", lambda e, bank=bank: e.matmul(bank[:, :ntok], lhsT=blk1[:], rhs=F(1), start=True, stop=True),
              r=["blk1", Fk(1)], w=[bk])
            A("dve", lambda e, bank=bank: e.tensor_scalar(out=F(1), in0=bank[:, :ntok], scalar1=1e-24, scalar2=None,
                                                          op0=ALU.max), r=[bk], w=[Fk(1)])
            A("act", lambda e: e.activation(out=F(1), in_=F(1), func=AF.Ln), r=[Fk(1)], w=[Fk(1)])
            A("act", lambda e: e.activation(out=F(1), in_=F(1), func=AF.Exp, scale=-0.5), r=[Fk(1)], w=[Fk(1)])
            A("dve", lambda e: e.tensor_tensor(out=F(0), in0=F(0), in1=F(1), op=ALU.mult), r=[Fk(0), Fk(1)], w=[Fk(0)])
            A("dve", lambda e: e.tensor_tensor(out=F(2), in0=F(0), in1=Ap, op=ALU.mult), r=[Fk(0), "At"], w=[Fk(2)])
            A("dve", lambda e: e.tensor_scalar(out=F(3), in0=Ap, scalar1=pc(3), scalar2=pv2[:, p:p + 1],
                                               op0=ALU.mult, op1=ALU.add), r=["At", "pvec", "pv2"], w=[Fk(3)])
            A("dve", lambda e: e.tensor_tensor(out=v3(Fs[3]), in0=v3(Fs[3]), in1=KR, op=ALU.mult), r=[Fk(3), XPk], w=[Fk(3)])
            A("dve", lambda e: e.tensor_tensor_scan(out=F(4), data0=scmask[:, :ntok], data1=SWp, initial=0.0,
                                                    op0=ALU.mult, op1=ALU.add), r=["scm", "SWt"], w=[Fk(4)])
            A("act", lambda e: e.activation(out=F(5), in_=F(4), func=AF.Exp, scale=-DEC), r=[Fk(4)], w=[Fk(5)])
            A("dve", lambda e: e.tensor_tensor(out=v3(Hs[0]), in0=R, in1=v3(Fs[5]), op=ALU.mult), r=[XPk, Fk(5)], w=[Hk(0)])
            A("act", lambda e: e.activation(out=F(5), in_=F(4), func=AF.Exp, scale=DEC), r=[Fk(4), Hk(0)], w=[Fk(5)])
            A("dve", lambda e: e.tensor_tensor(out=Hh(1), in0=F(2), in1=F(5), op=ALU.mult), r=[Fk(2), Fk(5)], w=[Hk(1)])
            A("dve", lambda e: e.tensor_tensor(out=Hh(2), in0=F(3), in1=F(5), op=ALU.mult), r=[Fk(3), Fk(5)], w=[Hk(2)])
            A("dve", lambda e: e.tensor_tensor(out=F(6), in0=F(4), in1=SWp, op=ALU.subtract), r=[Fk(4), "SWt"], w=[Fk(6)])
            A("act", lambda e: e.activation(out=F(6), in_=F(6), func=AF.Exp, scale=-DEC), r=[Fk(6)], w=[Fk(6)])
            A("dve", lambda e: e.scalar_tensor_tensor(out=Hh(3), in0=F(0), scalar=-1.0, in1=F(6), op0=ALU.mult, op1=ALU.mult),
              r=[Fk(0), Fk(6)], w=[Hk(3)])
            for ci in range(nch):
                cc = slice(ci * C, (ci + 1) * C)
                A("dve", lambda e, cc=cc, ci=ci: e.tensor_scalar(out=Fs[7][:, cc], in0=Fs[4][:, cc],
                                                                 scalar1=Fs[4][:, (ci + 1) * C - 1:(ci + 1) * C],
                                                                 scalar2=None, op0=ALU.subtract), r=[Fk(4)], w=[Fk(7)])
            A("act", lambda e: e.activation(out=F(7), in_=F(7), func=AF.Exp, scale=DEC), r=[Fk(7)], w=[Fk(7)])
            A("dve", lambda e: e.tensor_tensor(out=Hh(4), in0=F(2), in1=F(7), op=ALU.mult), r=[Fk(2), Fk(7)], w=[Hk(4)])
            A("dve", lambda e: e.tensor_tensor(out=Hh(5), in0=F(3), in1=F(7), op=ALU.mult), r=[Fk(3), Fk(7)], w=[Hk(5)])
            A("act", lambda e: e.activation(out=gct[:, 0:nch], in_=Fs[4][:, C - 1:ntok:C], func=AF.Exp, scale=-DEC),
              r=[Fk(4)], w=["gct"])
            A("act", lambda e: e.activation(out=v3(Hs[6]), in_=VR, func=AF.Copy), r=[XPk], w=[Hk(6)])
            for ci in range(nch):
                cc = slice(ci * C, (ci + 1) * C)
                for i_, hi in enumerate((6, 4, 5)):
                    A("pe", lambda e, cc=cc, i_=i_, hi=hi: e.transpose(out=tpb[:C, i_ * 128:(i_ + 1) * 128],
                                                                       in_=Hs[hi][:, cc], identity=ident_bf[:]),
                      r=[Hk(hi), "ident_bf"], w=["tpb"])
                A("dve", lambda e, ci=ci: e.tensor_copy(
                    out=VZ[:C, ci, :].rearrange("p (b d) -> p b d", d=64)[:, 0::2, :],
                    in_=tpb[:C, 0:128].rearrange("p (b d) -> p b d", d=64)), r=["tpb"], w=["VZ"])
                A("dve", lambda e, ci=ci: e.tensor_copy(out=BKt[:C, ci, :], in_=tpb[:C, 128:384]),
                  r=["tpb"], w=["BKt"])

            YT = 1
            for ci in range(nch):
                cc = slice(ci * C, (ci + 1) * C)
                si = p + 2 if sample else p
                if sample:
                    state_load(si, p, ci)
                elif blk["tok0"] == 0 and ci == 0:
                    state_zero(si)
                rkb = []
                for e_ in range(2):
                    sl = slice(e_ * 64, e_ * 64 + 64)
                    bank, bk = mm.get()
                    specs = [(1, 3), (3, 1), (2, 3), (1, 0)]
                    for i_, (li, ri) in enumerate(specs):
                        A("pe", lambda e, bank=bank, i_=i_, li=li, ri=ri, sl=sl, cc=cc: e.matmul(
                            bank[:C, i_ * 128:i_ * 128 + C], lhsT=Hs[li][sl, cc], rhs=Hs[ri][sl, cc],
                            start=True, stop=True), r=[Hk(li), Hk(ri)], w=[bk])
                    A("dve", lambda e, bank=bank, e_=e_: e.tensor_tensor(
                        out=M4[e_][:C, :, :C], in0=bank[:C, :].rearrange("p (a b) -> p a b", b=128)[:, :, :C],
                        in1=mk4[:C, :, :C], op=ALU.mult), r=[bk, "mk4"], w=[f"M4{e_}"])
                    bank2, bk2 = mm.get()
                    A("pe", lambda e, bank2=bank2, sl=sl, cc=cc: e.matmul(
                        bank2[:C, 0:C], lhsT=Hs[2][sl, cc], rhs=Hs[0][sl, cc], start=True, stop=True),
                      r=[Hk(2), Hk(0)], w=[bk2])
                    A("dve", lambda e, bank2=bank2, e_=e_: e.tensor_tensor(
                        out=RKm[:C, e_, :C], in0=bank2[:C, 0:C], in1=m_sui[:C, 0, :C], op=ALU.mult),
                      r=[bk2, "m_sui"], w=["RKm"])
                Xc = lambda e_: M4[e_][:C, 0, :C]
                XTc = lambda e_: M4[e_][:C, 1, :C]
                xkeys = ["M40", "M41"]
                A("dve", lambda e: e.tensor_tensor(out=PP[0][:C, 0, :C], in0=M4[0][:C, 0, :C], in1=ident_bf[:C, :C],
                                                   op=ALU.add), r=["M40", "ident_bf"], w=["PP0"])
                A("dve", lambda e: e.tensor_tensor(out=PP[0][:C, 1, :C], in0=M4[1][:C, 0, :C], in1=ident_bf[:C, :C],
                                                   op=ALU.add), r=["M41", "ident_bf"], w=["PP0"])
                pi = 0
                for k in range(1, nlev):
                    need_x = k <= nlev - 2
                    xi = k % 2
                    bank, bk = mm.get()
                    for e_ in range(2):
                        A("pe", lambda e, bank=bank, e_=e_, Xc=Xc, XTc=XTc: e.matmul(
                            bank[:C, (2 + e_) * 128:(2 + e_) * 128 + C], lhsT=Xc(e_), rhs=XTc(e_), start=True, stop=True),
                          r=xkeys, w=[bk])
                        if need_x:
                            A("pe", lambda e, bank=bank, e_=e_, Xc=Xc, XTc=XTc: e.matmul(
                                bank[:C, e_ * 128:e_ * 128 + C], lhsT=XTc(e_), rhs=Xc(e_), start=True, stop=True),
                              r=xkeys, w=[bk])
                    lo_ = 0 if need_x else 2
                    A("dve", lambda e, bank=bank, xi=xi, lo_=lo_: e.tensor_copy(
                        out=XX[xi][:C, lo_:4, :C],
                        in_=bank[:C, :].rearrange("p (a b) -> p a b", b=128)[:, lo_:4, :C]),
                      r=[bk], w=[f"XX{xi}"])
                    Xc = lambda e_, xi=xi: XX[xi][:C, e_, :C]
                    XTc = lambda e_, xi=xi: XX[xi][:C, 2 + e_, :C]
                    xkeys = [f"XX{xi}"]
                    bank, bk = mm.get()
                    for e_ in range(2):
                        A("pe", lambda e, bank=bank, e_=e_, XTc=XTc, pi=pi: e.matmul(
                            bank[:C, e_ * 128:e_ * 128 + C], lhsT=XTc(e_), rhs=PP[pi][:C, e_, :C], start=True, stop=True),
                          r=xkeys + [f"PP{pi}"], w=[bk])
                    A("dve", lambda e, bank=bank, pi=pi: e.tensor_tensor(
                        out=PP[1 - pi][:C, :, :C], in0=bank[:C, 0:256].rearrange("p (a b) -> p a b", b=128)[:, :, :C],
                        in1=PP[pi][:C, :, :C], op=ALU.add), r=[bk, f"PP{pi}"], w=[f"PP{1 - pi}"])
                    pi = 1 - pi
                Sb, Sbk, Sf, Sfk = S_bf[si], f"S_bf{si}", S_f[si], f"S_f{si}"
                bank, bk = mm.get()
                A("pe", lambda e, bank=bank, cc=cc: e.matmul(bank[:C, 0:128], lhsT=Hs[3][:, cc], rhs=Sb[:], start=True, stop=False,
                                                             skip_group_check=True), r=[Hk(3), Sbk], w=[bk])
                for e_ in range(2):
                    A("pe", lambda e, bank=bank, e_=e_, ci=ci: e.matmul(
                        bank[:C, e_ * 64:(e_ + 1) * 64], lhsT=M4[e_][:C, 2, :C], rhs=VZ[:C, ci, e_ * 128:e_ * 128 + 64],
                        start=False, stop=(e_ == 1), skip_group_check=True), r=[f"M4{e_}", "VZ"], w=[bk])
                A("dve", lambda e, bank=bank: e.tensor_copy(out=W1b[:C, :], in_=bank[:C, 0:128]), r=[bk], w=["W1b"])
                bank, bk = mm.get()
                for e_ in range(2):
                    A("pe", lambda e, bank=bank, e_=e_, pi=pi: e.matmul(
                        bank[:C, e_ * 64:(e_ + 1) * 64], lhsT=PP[pi][:C, e_, :C], rhs=W1b[:C, e_ * 64:(e_ + 1) * 64],
                        start=True, stop=True), r=[f"PP{pi}", "W1b"], w=[bk])
                A("dve", lambda e, bank=bank: e.tensor_copy(
                    out=UZ[:C, :].rearrange("p (b d) -> p b d", d=64)[:, 0::2, :],
                    in_=bank[:C, 0:128].rearrange("p (b d) -> p b d", d=64)), r=[bk], w=["UZ"])
                bank, bk = mm.get()
                A("pe", lambda e, bank=bank, cc=cc: e.matmul(bank[:, 0:C], lhsT=Sb[:], rhs=Hs[0][:, cc], start=True, stop=False,
                                                             skip_group_check=True), r=[Sbk, Hk(0)], w=[bk])
                for e_ in range(2):
                    A("pe", lambda e, bank=bank, e_=e_: e.matmul(
                        bank[:, 0:C], lhsT=UZ[:C, e_ * 64:e_ * 64 + 128], rhs=M4[e_][:C, 3, :C], start=False, stop=False,
                        skip_group_check=True), r=["UZ", f"M4{e_}"], w=[bk])
                    A("pe", lambda e, bank=bank, e_=e_, ci=ci: e.matmul(
                        bank[:, 0:C], lhsT=VZ[:C, ci, e_ * 64:e_ * 64 + 128], rhs=RKm[:C, e_, :C], start=False,
                        stop=(e_ == 1), skip_group_check=True), r=["VZ", "RKm"], w=[bk])
                A("dve", lambda e, bank=bank, cc=cc: e.tensor_copy(out=Fs[YT][:, cc], in_=bank[:, 0:C]),
                  r=[bk], w=[Fk(YT)])
                bank, bk = mm.get()
                A("pe", lambda e, bank=bank, ci=ci: e.matmul(
                    bank[:, 0:128], lhsT=BKt[:C, ci, 0:128], rhs=UZ[:C, :].rearrange("p (b d) -> p b d", d=64)[:, 0::2, :],
                    start=True, stop=False, skip_group_check=True), r=["BKt", "UZ"], w=[bk])
                A("pe", lambda e, bank=bank, ci=ci: e.matmul(
                    bank[:, 0:128], lhsT=BKt[:C, ci, 128:256],
                    rhs=VZ[:C, ci, :].rearrange("p (b d) -> p b d", d=64)[:, 0::2, :],
                    start=False, stop=True, skip_group_check=True), r=["BKt", "VZ"], w=[bk])
                for e_ in range(2):
                    sl = slice(e_ * 64, e_ * 64 + 64)
                    A("dve", lambda e, bank=bank, sl=sl, ci=ci: e.scalar_tensor_tensor(
                        out=Sf[sl, sl], in0=Sf[sl, sl], scalar=gct[sl, ci:ci + 1], in1=bank[sl, sl],
                        op0=ALU.mult, op1=ALU.add), r=[bk, Sfk, "gct"], w=[Sfk])
                for e_ in range(2):
                    sl = slice(e_ * 64, e_ * 64 + 64)
                    A("pool", lambda e, sl=sl: e.tensor_copy(out=Sb[sl, sl], in_=Sf[sl, sl]), r=[Sfk], w=[Sbk])
                if sample:
                    state_store(si, p, 1 + ci)
                elif blk["tok0"] + 512 >= min(nblk, 8) * 512 and ci == nch - 1:
                    state_store(si, p, 0)

            bank, bk = mm.get()
            A("pe", lambda e, bank=bank: e.matmul(bank[:, :ntok], lhsT=blk64[:], rhs=F(YT), start=True, stop=True),
              r=["blk64", Fk(YT)], w=[bk])
            A("dve", lambda e, bank=bank: e.tensor_tensor(out=F(5), in0=F(YT), in1=bank[:, :ntok], op=ALU.subtract),
              r=[bk, Fk(YT)], w=[Fk(5)])
            A("act", lambda e: e.activation(out=F(6), in_=F(5), func=AF.Square), r=[Fk(5)], w=[Fk(6)])
            bank, bk = mm.get()
            A("pe", lambda e, bank=bank: e.matmul(bank[:, :ntok], lhsT=blk64[:], rhs=F(6), start=True, stop=True),
              r=["blk64", Fk(6)], w=[bk])
            A("act", lambda e, bank=bank: e.activation(out=F(6), in_=bank[:, :ntok], func=AF.Ln, bias=64e-5), r=[bk], w=[Fk(6)])
            A("act", lambda e: e.activation(out=F(6), in_=F(6), func=AF.Exp, scale=-0.5), r=[Fk(6)], w=[Fk(6)])
            A("dve", lambda e: e.tensor_tensor(out=F(5), in0=F(5), in1=F(6), op=ALU.mult), r=[Fk(5), Fk(6)], w=[Fk(5)])
            A("dve", lambda e: e.tensor_scalar(out=F(5), in0=F(5), scalar1=pc(5), scalar2=pc(6), op0=ALU.mult, op1=ALU.add),
              r=[Fk(5), "pvec"], w=[Fk(5)])
            A("dve", lambda e: e.scalar_tensor_tensor(out=v3(Fs[7]), in0=R, scalar=pc(4), in1=v3(Fs[3]), op0=ALU.mult,
                                                      op1=ALU.mult), r=[XPk, Fk(3), "pvec"], w=[Fk(7)])
            bank, bk = mm.get()
            A("pe", lambda e, bank=bank: e.matmul(bank[:, :ntok], lhsT=blk1[:], rhs=F(7), start=True, stop=True),
              r=["blk1", Fk(7)], w=[bk])
            A("dve", lambda e, bank=bank: e.tensor_tensor(
                out=v3(Fs[7]), in0=bank[:, :ntok].rearrange("p (s l) -> p s l", l=L), in1=VR, op=ALU.mult),
              r=[bk, XPk], w=[Fk(7)])
            A("dve", lambda e: e.tensor_tensor(out=F(5), in0=F(5), in1=F(7), op=ALU.add), r=[Fk(5), Fk(7)], w=[Fk(5)])
            ob, obk = orw.get()
            A("dve", lambda e, ob=ob: e.tensor_tensor(out=ob[:, :ntok], in0=F(5), in1=Gt[:, p, :ntok], op=ALU.mult),
              r=[Fk(5), "Gt"], w=[obk])
            if sample:
                for sq in range(2):
                    mix_store(ob[:, sq * NS:(sq + 1) * NS], obk, 2 + p, T + sq * NS, NS)
            else:
                mix_store(ob[:, :ntok], obk, 2 + p, blk["tok0"], ntok)

        blocks = []
        for tb in range(8):
            blocks.append(dict(tok0=tb * 512, ntok=512, sample=False,
                               tiles=[(tb * 512 + i * 128, 128) for i in range(4)]))
        blocks.append(dict(tok0=T, ntok=2 * NS, sample=True, tiles=[(T, NS), (T + NS, NS)]))
        nblk = int(os.environ.get("K_NBLK", "9"))
        do_attn = stage >= 2

        EVERY = int(os.environ.get("K_EVERY", "6"))
        todo = [b for i, b in enumerate(blocks) if stage > 0 and (b["sample"] or i < nblk)]
        if os.environ.get("K_NOSAMPLE"):
            todo = [b for b in todo if not b["sample"]]
        OVL = not os.environ.get("K_NOOVL")

        def part1(blk):
            hTt, hTk = project_block(blk)
            qTt, qTk = qT.get()
            proj_feature_major(blk, hTt, hTk, qTt, qTk, "qk")
            proj_token_major(blk, hTt, hTk)
            return (hTt, hTk, qTt, qTk)

        def part2(blk, hq):
            proj_feature_major(blk, hq[0], hq[1], hq[2], hq[3], "rw")

        def rwkv_entry(blk):
            sink[0] = []
            rwkv_block(blk)
            items = sink[0]
            sink[0] = None
            ym = os.environ.get("K_YMODE")
            if ym == "evac":
                nst = sum(1 for it in items if it[0] != "pe" and any(k in PSUM_KEYS for k in it[2])) // int(os.environ.get("K_EVMIN", "2")) + 1
                return [replay_items(items, EVERY, "evac"), nst]
            return [replay_items(items, EVERY), len(items) // EVERY + 1]

        def part1_entry(blk):
            sink[0] = []
            hq_ = part1(blk)
            items1 = sink[0]
            sink[0] = None
            ev1 = int(os.environ.get("K_EVERY_P1", "16"))
            return hq_, [replay_items(items1, ev1), len(items1) // ev1 + 1]

        prompts = [b for b in todo if not b["sample"]]
        samp = [b for b in todo if b["sample"]]
        samp = samp[0] if samp else None
        s_at = min(3, len(prompts) - 1) if (samp is not None and prompts) else None
        hq = None
        if prompts:
            hq = part1(prompts[0])
            part2(prompts[0], hq)
        elif samp is not None:
            hq_s = part1(samp)
            part2(samp, hq_s)
            interleave([rwkv_entry(samp)] if stage >= 3 else [] + ([[attention_sample(hq_s[2], hq_s[3]), 68]] if do_attn else []))
            if stage >= 3 and do_attn:
                interleave([[attention_sample(hq_s[2], hq_s[3]), 68]])
        for ti_, blk in enumerate(prompts):
            bi = blocks.index(blk)
            qTt, qTk = hq[2], hq[3]
            nxt = prompts[ti_ + 1] if ti_ + 1 < len(prompts) else None
            if s_at is not None and ti_ == s_at:
                hq_s = part1(samp)
                part2(samp, hq_s)
                ent1 = []
                if stage >= 3:
                    ent1.append(rwkv_entry(blk))
                if do_attn:
                    ent1.append([attention_sample(hq_s[2], hq_s[3]), 2 * 2 * 17])
                interleave(ent1)
                ent2 = []
                if do_attn:
                    ent2.append([attention_prompt(bi, qTt, qTk), int(16 * (bi + 1) * float(os.environ.get("K_ATTW", "1.4")))])
                if stage >= 3:
                    ent2.append(rwkv_entry(samp))
                hq_n = None
                if nxt is not None and OVL:
                    qT.get()
                    hq_n, e1 = part1_entry(nxt)
                    ent2.append(e1)
                interleave(ent2)
            else:
                entries = []
                if stage >= 3:
                    entries.append(rwkv_entry(blk))
                if do_attn:
                    entries.append([attention_prompt(bi, qTt, qTk), int(16 * (bi + 1) * float(os.environ.get("K_ATTW", "1.4")))])
                hq_n = None
                if nxt is not None and OVL:
                    hq_n, e1 = part1_entry(nxt)
                    entries.append(e1)
                interleave(entries)
            if nxt is not None:
                if hq_n is None:
                    hq_n = part1(nxt)
                part2(nxt, hq_n)
                hq = hq_n

        shst = sb("shst", [128, 3, 9])
        A("pool", lambda e: e.memset(shst[:].rearrange("p a b -> p (a b)"), 0.0), w=["shst"])
        lc_p, lc_s = (0, 0) if stage >= 3 else (512, NS)
        A("pool", lambda e: e.tensor_copy(out=shst[:, 0, :], in_=XPp[:, :, 0, lc_p]), r=["XP"], w=["shst"])
        for sq in range(2):
            A("pool", lambda e, sq=sq: e.tensor_copy(out=shst[:, 1 + sq, :], in_=XPs[:, :, sq, lc_s]),
              r=["XPs"], w=["shst"])
        out_toks.append(A(OUTQ, lambda e: e.dma_start(out=sho, in_=shst[:]), r=["shst"], dma="osh"))

        if dev and do_attn:
            pieces = [(0, min(nblk, 8) * 512), (T, 2 * NS)]
            nch = 4 if stage >= 3 else 2
            for (c0_, n_) in [pp for pp in pieces if pp[1] > 0]:
                out_toks.append(A("pool", lambda e, c0_=c0_, n_=n_: e.dma_start(
                    out=dbg_mix[0:nch * 128, c0_:c0_ + n_], in_=mixb_in[0:nch * 128, c0_:c0_ + n_]),
                    r=["mixb_in"], dma="dbg"))


        if stage >= 4:
            for c in range(4):
                A("pool", lambda e, c=c: e.collective_compute(
                    "AllGather", ALU.bypass, replica_groups=[[0, 1], [2, 3], [4, 5], [6, 7]],
                    ins=[mixb_in[c * 128:(c + 1) * 128, :]], outs=[mixb_outs[c][:, :]]),
                  r=["mixb_in"], w=["mixb_out"], cc=f"ag{c}")
            P.barrier()
            P.emit(st)
            stA.close()
            cur["st"] = st
            TB = 256
            gffn = sb("gffn", [128, D])
            gfin = sb("gfin", [128, D])
            A("sp", lambda e: e.dma_start(out=gffn[:], in_=g_ffn), w=["gffn"], dma="c0d")
            A("sp", lambda e: e.dma_start(out=gfin[:], in_=g_fin), w=["gfin"], dma="c0e")
            wout_bf = sb("wout_bf", [128, 8, D], BF16)
            wg_bf = sb("wg_bf", [128, 8, FF], BF16)
            wu_bf = sb("wu_bf", [128, 8, FF], BF16)
            wd_bf = sb("wd_bf", [128, 22, D], BF16)
            mA = sb("mA", [128, 8, TB], BF16)
            mB = sb("mB", [128, 8, TB], BF16)
            mS = mA
            xn = [sb(f"xn{i}", [128, D]) for i in range(2)]
            wstg = Rot("xn", xn)
            xbt = Rot("xbt", [sb(f"xbt{i}", [128, D]) for i in range(2)])
            h2b = Rot("h2b", [sb(f"h2b{i}", [128, D], BF16) for i in range(2)])
            h2T = sb("h2T", [128, 8, TB], BF16)
            hidT = sb("hidT", [128, 22, TB], BF16)
            sil = Rot("sil", [sb(f"sil{i}", [128, TB]) for i in range(2)])
            statb = Rot("statb", [sb(f"statb{i}", [128, 4]) for i in range(2)])
            ytile = Rot("ytile", [sb(f"ytile{i}", [128, D]) for i in range(1)])
            mmB = Rot("mmB", mm.bufs + Xb.bufs + oTb.bufs)
            mmB_keys = ["mm0", "mm1", "mm2", "X0", "X1", "oT0", "oT1"]

            def bankB():
                i = mmB.i % 7
                mmB.i += 1
                return mmB.bufs[i], mmB_keys[i]

            cast_engs = ["dve", "pool", "act"]
            cast_i = [0]

            def load_cast(dst_fn, src_ap_fn, nrow_chunks, ncols, wkey):
                for kc in range(nrow_chunks):
                    for c0 in range(0, ncols, 1024):
                        w_ = min(1024, ncols - c0)
                        stg, sk = wstg.get()
                        A("sp", lambda e, stg=stg, kc=kc, c0=c0, w_=w_: e.dma_start(
                            out=stg[:, 0:w_], in_=src_ap_fn(kc)[:, c0:c0 + w_]), w=[sk], dma=sk)
                        eng = cast_engs[cast_i[0] % 3]
                        cast_i[0] += 1
                        if eng == "act":
                            A("act", lambda e, stg=stg, kc=kc, c0=c0, w_=w_: e.activation(
                                out=dst_fn(kc)[:, c0:c0 + w_], in_=stg[:, 0:w_], func=AF.Copy), r=[sk], w=[wkey])
                        else:
                            A(eng, lambda e, stg=stg, kc=kc, c0=c0, w_=w_: e.tensor_copy(
                                out=dst_fn(kc)[:, c0:c0 + w_], in_=stg[:, 0:w_]), r=[sk], w=[wkey])

            def load_cast2(dst_fn, src_ap_fn, kcs, c0, c1, wkey):
                for kc in kcs:
                    A("pool", lambda e, kc=kc: e.dma_start(out=dst_fn(kc)[:, c0:c1], in_=src_ap_fn(kc)[:, c0:c1]),
                      w=[wkey], dma=wkey)

            NCB = 4
            CBW = FF // NCB
            load_cast2(lambda kc: wout_bf[:, kc, :], lambda kc: wout_d[kc * 128:(kc + 1) * 128, :], range(8), 0, D, "wout_bf")
            for cb in range(NCB):
                load_cast2(lambda kc: wg_bf[:, kc, :], lambda kc: wg_d[kc * 128:(kc + 1) * 128, :], range(8),
                           cb * CBW, (cb + 1) * CBW, f"wg_bf{cb}")
                load_cast2(lambda kc: wu_bf[:, kc, :], lambda kc: wu_d[kc * 128:(kc + 1) * 128, :], range(8),
                           cb * CBW, (cb + 1) * CBW, f"wu_bf{cb}")
            for ht in range(22):
                load_cast2(lambda kc: wd_bf[:, kc, :], lambda kc: wd_d[kc * 128:(kc + 1) * 128, :], [ht], 0, D, f"wd_bf{ht}")

            def wkeys(prefix, ht):
                a, b = (ht * 128) // CBW, (ht * 128 + 127) // CBW
                return [f"{prefix}{a}"] if a == b else [f"{prefix}{a}", f"{prefix}{b}"]

            mos = [mixb_outs[c][:, :].rearrange("(r p) t -> p r t", p=128) for c in range(4)]

            def norm_tile(src, srck, nrows, gt, gk, dst, dstk):
                stt, sk = statb.get()
                A("act", lambda e: e.activation(out=dst[:nrows, :], in_=src[:nrows, :], func=AF.Square,
                                                accum_out=stt[:nrows, 0:1]), r=[srck], w=[dstk, sk])
                A("act", lambda e: e.activation(out=stt[:nrows, 1:2], in_=stt[:nrows, 0:1], func=AF.Ln,
                                                scale=1.0 / D, bias=1e-6), r=[sk], w=[sk])
                A("act", lambda e: e.activation(out=stt[:nrows, 2:3], in_=stt[:nrows, 1:2], func=AF.Exp,
                                                scale=-0.5), r=[sk], w=[sk])
                A("dve", lambda e: e.scalar_tensor_tensor(out=dst[:nrows, :], in0=src[:nrows, :],
                                                          scalar=stt[:nrows, 2:3], in1=gt[:nrows, :],
                                                          op0=ALU.mult, op1=ALU.mult), r=[srck, sk, gk], w=[dstk])

            bblocks = [(i * TB, TB, i * TB, T // 2 + i * TB) for i in range(T // 2 // TB)]
            bblocks.append((T // 2, NS, T, T + NS))
            h2bs = h2b.bufs

            def tiles_of(ntok):
                return [(t0, min(128, ntok - t0)) for t0 in range(0, ntok, 128)]

            def prep(j):
                row0, ntok, colA, colB = bblocks[j]
                for c in range(4):
                    A("sp", lambda e, c=c, colA=colA, ntok=ntok: e.dma_start(
                        out=mA[:, c::4, :ntok], in_=mos[c][:, :, colA:colA + ntok]), r=["mixb_out"], w=["mA"], dma="mA")
                    A("sp", lambda e, c=c, colB=colB, ntok=ntok: e.dma_start(
                        out=mB[:, c::4, :ntok], in_=mos[c][:, :, colB:colB + ntok]), r=["mixb_out"], w=["mB"], dma="mB")
                A("dve", lambda e, ntok=ntok: e.tensor_scalar(out=mA[:, :, :ntok], in0=mA[:, :, :ntok],
                                                              scalar1=rsel[:, 0:1], scalar2=None, op0=ALU.mult),
                  r=["mA", "rsel"], w=["mA"])
                A("dve", lambda e, ntok=ntok: e.scalar_tensor_tensor(out=mA[:, :, :ntok], in0=mB[:, :, :ntok],
                                                                      scalar=rsel[:, 1:2], in1=mA[:, :, :ntok],
                                                                      op0=ALU.mult, op1=ALU.add),
                  r=["mA", "mB", "rsel"], w=["mA"])

            def s1(j):
                row0, ntok, colA, colB = bblocks[j]
                tl = tiles_of(ntok)
                xbs = []
                for ti, (t0, nr) in enumerate(tl):
                    xb_t, xbk = xbt.get()
                    A("sp", lambda e, xb_t=xb_t, t0=t0, nr=nr, row0=row0: e.dma_start(
                        out=xb_t[:nr, :], in_=xb[row0 + t0:row0 + t0 + nr, :]), w=[xbk], dma=xbk)
                    xbs.append((xb_t, xbk))
                banks = []
                for ti, (t0, nr) in enumerate(tl):
                    for hf in range(2):
                        bank, bk = bankB()
                        for kc in range(8):
                            A("pe", lambda e, bank=bank, kc=kc, hf=hf, t0=t0, nr=nr: e.matmul(
                                bank[:nr, :], lhsT=mA[:, kc, t0:t0 + nr], rhs=wout_bf[:, kc, hf * 512:(hf + 1) * 512],
                                start=(kc == 0), stop=(kc == 7)), r=["mA", "wout_bf"], w=[bk])
                        banks.append((bank, bk))
                for ti, (t0, nr) in enumerate(tl):
                    xnt, xnk = xn[ti], f"xn{ti}"
                    xb_t, xbk = xbs[ti]
                    for hf in range(2):
                        bank, bk = banks[ti * 2 + hf]
                        A("dve", lambda e, bank=bank, hf=hf, xnt=xnt, xb_t=xb_t, nr=nr: e.tensor_tensor(
                            out=xnt[:nr, hf * 512:(hf + 1) * 512], in0=bank[:nr, :], in1=xb_t[:nr, hf * 512:(hf + 1) * 512],
                            op=ALU.add), r=[bk, xbk], w=[xnk])
                    norm_tile(xnt, xnk, nr, gffn, "gffn", h2bs[ti], f"h2b{ti}")

            def s1b(j):
                row0, ntok, colA, colB = bblocks[j]
                for ti, (t0, nr) in enumerate(tiles_of(ntok)):
                    hb2, hbk2 = h2bs[ti], f"h2b{ti}"
                    for kc in range(8):
                        A("pe", lambda e, kc=kc, hb2=hb2, nr=nr: e.transpose(
                            out=tpb[:, kc * 128:kc * 128 + nr], in_=hb2[:nr, kc * 128:(kc + 1) * 128],
                            identity=ident_bf[:nr, :nr]), r=[hbk2, "ident_bf"], w=["tpb"])
                    A("act", lambda e, t0=t0, nr=nr: e.activation(
                        out=h2T[:, :, t0:t0 + nr], in_=tpb[:].rearrange("p (k t) -> p k t", t=128)[:, :, 0:nr],
                        func=AF.Copy), r=["tpb"], w=["h2T"])

            def s2(j):
                row0, ntok, colA, colB = bblocks[j]
                for ht in range(22):
                    hs_ = slice(ht * 128, (ht + 1) * 128)
                    gb, gbk = bankB()
                    for kc in range(8):
                        A("pe", lambda e, gb=gb, kc=kc, hs_=hs_, ntok=ntok: e.matmul(
                            gb[:, :ntok], lhsT=wg_bf[:, kc, hs_], rhs=h2T[:, kc, :ntok], start=(kc == 0), stop=(kc == 7)),
                          r=wkeys("wg_bf", ht) + ["h2T"], w=[gbk])
                    ub, ubk = bankB()
                    for kc in range(8):
                        A("pe", lambda e, ub=ub, kc=kc, hs_=hs_, ntok=ntok: e.matmul(
                            ub[:, :ntok], lhsT=wu_bf[:, kc, hs_], rhs=h2T[:, kc, :ntok], start=(kc == 0), stop=(kc == 7)),
                          r=wkeys("wu_bf", ht) + ["h2T"], w=[ubk])
                    st_, stk_ = sil.get()
                    A("act", lambda e, st_=st_, gb=gb, ntok=ntok: e.activation(out=st_[:, :ntok], in_=gb[:, :ntok],
                                                                               func=AF.Silu), r=[gbk], w=[stk_])
                    A("dve", lambda e, st_=st_, ub=ub, ht=ht, ntok=ntok: e.tensor_tensor(
                        out=hidT[:, ht, :ntok], in0=ub[:, :ntok], in1=st_[:, :ntok], op=ALU.mult),
                      r=[ubk, stk_], w=["hidT"])

            def s3(j):
                row0, ntok, colA, colB = bblocks[j]
                for ti, (t0, nr) in enumerate(tiles_of(ntok)):
                    xnt, xnk = xn[ti], f"xn{ti}"
                    for hf in range(2):
                        bank, bk = bankB()
                        for ht in range(22):
                            A("pe", lambda e, bank=bank, ht=ht, hf=hf, t0=t0, nr=nr: e.matmul(
                                bank[:nr, :], lhsT=hidT[:, ht, t0:t0 + nr], rhs=wd_bf[:, ht, hf * 512:(hf + 1) * 512],
                                start=(ht == 0), stop=(ht == 21)), r=["hidT", f"wd_bf{ht}"], w=[bk])
                        A("dve", lambda e, bank=bank, hf=hf, xnt=xnt, nr=nr: e.tensor_tensor(
                            out=xnt[:nr, hf * 512:(hf + 1) * 512], in0=bank[:nr, :], in1=xnt[:nr, hf * 512:(hf + 1) * 512],
                            op=ALU.add), r=[bk, xnk], w=[xnk])
                    yt, ytk = ytile.get()
                    norm_tile(xnt, xnk, nr, gfin, "gfin", yt, ytk)
                    out_toks.append(A(OUTQ, lambda e, yt=yt, t0=t0, nr=nr, row0=row0: e.dma_start(
                        out=yo[row0 + t0:row0 + t0 + nr, :], in_=yt[:nr, :]), r=[ytk], dma="o" + ytk))

            nbb = len(bblocks)
            prep(0)
            for j in range(nbb):
                s1(j)
                if j + 1 < nbb:
                    prep(j + 1)
                s1b(j)
                s2(j)
                s3(j)

        P.wait_all(OUTQ, [t.tok if isinstance(t, PH) else t for t in out_toks])
        P.barrier()
        P.emit(st)
        if stage < 4:
            stA.close()
    return nc


def _core_inputs(inp, c):
    b, j = c // 2, c % 2
    f = lambda a: np.ascontiguousarray(a, dtype=np.float32)
    x_prompt, x_sample = inp["x_prompt"], inp["x_sample"]
    xa = np.concatenate([x_prompt[b], x_sample[2 * b], x_sample[2 * b + 1]], axis=0)
    xb = np.concatenate([x_prompt[b, j * 2048:(j + 1) * 2048], x_sample[2 * b + j]], axis=0)
    hs = slice(j * 256, (j + 1) * 256)
    w_in = inp["w_in"][0]
    sbw = 512
    rw0 = 3 * sbw
    cols = np.concatenate([
        np.arange(0, 512)[hs], 512 + np.arange(0, 512)[hs], 1024 + np.arange(0, 512)[hs],
        rw0 + np.arange(0, 512)[hs], rw0 + 512 + np.arange(0, 512)[hs], rw0 + 1024 + np.arange(0, 512)[hs],
        rw0 + 1536 + np.arange(0, 288)])
    win = w_in[:, cols]
    rcols = cols[768:] - rw0
    mu = inp["mu_shift"][0][rcols]

    def tiles9(v):
        o = np.zeros((128, 9), np.float32)
        for i in range(8):
            o[:, i] = v[i * 128:(i + 1) * 128]
        o[:32, 8] = v[1024:1056]
        return o

    pvec = np.zeros((128, 24), np.float32)
    pvec[:, 0:9] = tiles9(mu)
    names = ["w0", "a0", "k_k", "k_a", "r_k", "ln_x_w", "ln_x_b"]
    for p in range(2):
        for i, n in enumerate(names):
            v = inp[n][0].reshape(-1)[hs]
            pvec[:, 9 + p * 7 + i] = v[p * 128:(p + 1) * 128]
    wlo = np.concatenate([inp["w_decay_up"][0][:, hs], inp["w_aaa_up"][0][:, hs]], axis=0)
    wgu = inp["w_gate_up"][0][:, hs]
    ck = inp["cache_k"][0][2 * b:2 * b + 2, :, 4 * j:4 * j + 4, :].reshape(2, 2048, 256)
    cv = inp["cache_v"][0][2 * b:2 * b + 2, :, 4 * j:4 * j + 4, :].reshape(2, 2048, 256)
    swkv = inp["state_wkv"][0][2 * b:2 * b + 2, 4 * j:4 * j + 4]
    ssh = inp["state_shift"][0][2 * b:2 * b + 2, 0][:, rcols]
    sshift = np.stack([tiles9(ssh[0]), tiles9(ssh[1])], axis=1)
    perm = []
    for jj in range(2):
        for kind in range(2):
            for p in range(2):
                perm.append(kind * 512 + jj * 256 + p * 128 + np.arange(128))
    perm = np.concatenate(perm)
    wout = inp["w_out"][0][perm, :]
    rsel = np.zeros((128, 2), np.float32)
    rsel[:, j] = 1.0
    rep = lambda v: np.broadcast_to(v.reshape(1, -1), (128, v.size))
    return {
        "xa": f(xa), "xb": f(xb), "win": f(win),
        "g_mix": f(rep(inp["norm_mix_g"][0])), "g_ffn": f(rep(inp["norm_ffn_g"][0])),
        "g_fin": f(rep(inp["norm_final_g"])),
        "pvec": f(pvec), "wlo": f(wlo), "wgu": f(wgu), "ck": f(ck), "cv": f(cv),
        "swkv": f(swkv), "sshift": f(sshift), "wout": f(wout),
        "wg": f(inp["w_gate"][0]), "wu": f(inp["w_up"][0]), "wd": f(inp["w_down"][0]),
        "rsel": f(rsel),
    }


_NC_CACHE = {}


def run(inputs, stage=int(os.environ.get("K_STAGE_DEFAULT", "4")), dev=False):
    inp = {k: np.asarray(v) for k, v in inputs.items()}
    key = (stage, dev)
    if key not in _NC_CACHE:
        _NC_CACHE[key] = build(stage, dev)
    nc = _NC_CACHE[key]
    in_maps = [_core_inputs(inp, c) for c in range(8)]
    res = run_bass_kernel_spmd(nc, in_maps, core_ids=list(range(8)))
    return res.results


def kernel(**inputs):
    r = run(inputs)
    B, S, H, Dh = 4, T, 8, 64
    y_prompt = np.zeros((B, S, D), np.float32)
    y_sample = np.zeros((8, NS, D), np.float32)
    pk = np.zeros((1, B, S, H, Dh), np.float32)
    pv = np.zeros_like(pk)
    pS = np.zeros((1, B, H, Dh, Dh), np.float32)
    psh = np.zeros((1, B, 1, 1824), np.float32)
    sk = np.zeros((1, 8, NS, H, Dh), np.float32)
    sv = np.zeros_like(sk)
    sS = np.zeros((1, 8, H, Dh, Dh), np.float32)
    ssh = np.zeros((1, 8, 1, 1824), np.float32)
    for c in range(8):
        b, j = c // 2, c % 2
        o = r[c]
        hs = slice(4 * j, 4 * j + 4)
        pk[0, b, :, hs, :] = o["ko"][:T].reshape(T, 4, Dh)
        pv[0, b, :, hs, :] = o["vo"][:T].reshape(T, 4, Dh)
        for s in range(2):
            rows = slice(T + s * NS, T + (s + 1) * NS)
            sk[0, 2 * b + s, :, hs, :] = o["ko"][rows].reshape(NS, 4, Dh)
            sv[0, 2 * b + s, :, hs, :] = o["vo"][rows].reshape(NS, 4, Dh)
        y_prompt[b, j * 2048:(j + 1) * 2048] = o["yo"][:2048]
        y_sample[2 * b + j] = o["yo"][2048:]
        pS[0, b, hs] = o["wkvo"][0]
        sS[0, 2 * b, hs] = o["wkvo"][1]
        sS[0, 2 * b + 1, hs] = o["wkvo"][2]
        sho = o["sho"]
        for which, (dst, bi) in enumerate([(psh, b), (ssh, 2 * b), (ssh, 2 * b + 1)]):
            t9 = sho[:, which, :]
            for i, base in enumerate([0, 512, 1024]):
                for p in range(2):
                    dst[0, bi, 0, base + j * 256 + p * 128: base + j * 256 + (p + 1) * 128] = t9[:, 2 * i + p]
            dst[0, bi, 0, 1536:1664] = t9[:, 6]
            dst[0, bi, 0, 1664:1792] = t9[:, 7]
            dst[0, bi, 0, 1792:1824] = t9[:32, 8]
    return (y_prompt, y_sample, pk, pv, pS, psh, sk, sv, sS, ssh)
```

```python
import contextlib
import math
import os

import numpy as np
import concourse.bass as bass
import concourse.mybir as mybir
from concourse.bass_utils import run_bass_kernel_spmd

F32 = mybir.dt.float32
BF16 = mybir.dt.bfloat16
AF = mybir.ActivationFunctionType
ALU = mybir.AluOpType

ENGS = ("pe", "act", "dve", "pool", "sp")

T = 4096
NS = 32
TT = T + 2 * NS
TH = T // 2 + NS
D = 1024
FF = 2816
NCOL = 1824
DEC = math.exp(-0.5)


class Prog:
    def __init__(self, nc, sync_same_engine=True):
        self.nc = nc
        self.streams = {e: [] for e in ENGS}
        self.cnt = {e: 0 for e in ENGS}
        self.waited = {e: {} for e in ENGS}
        self.last_w = {}
        self.readers = {}
        self.dma_cnt = {}
        self.sync_same = sync_same_engine

    def op(self, eng, fn, reads=(), writes=(), dma=None, cc=None):
        deps = []
        for k in reads:
            t = self.last_w.get(k)
            if t is not None:
                deps.append(t)
        for k in writes:
            t = self.last_w.get(k)
            if t is not None:
                deps.append(t)
            deps.extend(self.readers.get(k, ()))
        if cc is not None:
            key = ("cc", cc)
            self.dma_cnt[key] = self.dma_cnt.get(key, 0) + 1
            tok = (key, self.dma_cnt[key])
        elif dma is not None:
            key = ("dma", dma)
            self.dma_cnt[key] = self.dma_cnt.get(key, 0) + 16
            tok = (key, self.dma_cnt[key])
        else:
            self.cnt[eng] += 1
            tok = (eng, self.cnt[eng])
        need = {}
        for (sk, v) in deps:
            if sk == eng and (eng == "pe" or not self.sync_same):
                continue
            if v > need.get(sk, 0):
                need[sk] = v
        waits = []
        wd = self.waited[eng]
        for sk, v in need.items():
            if wd.get(sk, 0) >= v:
                continue
            wd[sk] = v
            waits.append((sk, v))
        self.streams[eng].append((waits, fn, tok))
        for k in reads:
            self.readers.setdefault(k, []).append(tok)
        for k in writes:
            self.last_w[k] = tok
            self.readers[k] = []
        return tok

    def wait_all(self, eng, toks):
        need = {}
        for (sk, v) in toks:
            if v > need.get(sk, 0):
                need[sk] = v
        self.streams[eng].append((list(need.items()), None, None))

    def barrier(self):
        toks = []
        for e in ENGS[:4]:
            if self.cnt[e]:
                toks.append((e, self.cnt[e]))
        for k, v in self.dma_cnt.items():
            toks.append((k, v))
        for e in ENGS:
            need = {}
            wd = self.waited[e]
            for sk, v in toks:
                if sk == e or wd.get(sk, 0) >= v:
                    continue
                wd[sk] = v
                need[sk] = v
            self.streams[e].append((list(need.items()), None, None))

    def emit(self, st):
        nc = self.nc
        if not hasattr(self, "sems"):
            self.sems = {}
        sems = self.sems
        keys = list(ENGS[:4]) + list(self.dma_cnt.keys())
        for k in keys:
            if k not in sems:
                sems[k] = st.enter_context(nc.semaphore(f"s{len(sems)}"))
        streams = self.streams
        self.streams = {e: [] for e in ENGS}

        def replay(stream, e):
            for waits, fn, tok in stream:
                for sk, v in waits:
                    e.wait_ge(sems[sk], v)
                if fn is None:
                    continue
                inst = fn(e)
                if isinstance(tok[0], tuple):
                    if tok[0][0] == "cc":
                        inst.then_inc(sems[tok[0]])
                    else:
                        inst.then_inc(sems[tok[0]], 16)
                else:
                    inst.then_inc(sems[tok[0]], 1)

        with nc.Block() as block:
            @block.tensor
            def _(e):
                replay(streams["pe"], e)

            @block.scalar
            def _(e):
                replay(streams["act"], e)

            @block.vector
            def _(e):
                replay(streams["dve"], e)

            @block.gpsimd
            def _(e):
                replay(streams["pool"], e)

            @block.sync
            def _(e):
                replay(streams["sp"], e)


class Rot:
    def __init__(self, name, bufs):
        self.name = name
        self.bufs = bufs
        self.i = 0

    def get(self):
        i = self.i % len(self.bufs)
        self.i += 1
        return self.bufs[i], f"{self.name}{i}"


def build(stage=99, dev=False):
    nc = bass.Bass("TRN2", target_bir_lowering=False)
    din = lambda n, s: nc.dram_tensor(n, s, F32, kind="ExternalInput").ap()
    dout = lambda n, s: nc.dram_tensor(n, s, F32, kind="ExternalOutput").ap()
    xa = din("xa", [TT, D])
    xb = din("xb", [TH, D])
    win = din("win", [D, NCOL])
    g_mix = din("g_mix", [128, D])
    g_ffn = din("g_ffn", [128, D])
    g_fin = din("g_fin", [128, D])
    pvec_d = din("pvec", [128, 24])
    wlo_d = din("wlo", [128, 256])
    wgu_d = din("wgu", [160, 256])
    ck_d = din("ck", [2, 2048, 256])
    cv_d = din("cv", [2, 2048, 256])
    swkv_d = din("swkv", [2, 4, 64, 64])
    sshift_d = din("sshift", [128, 2, 9])
    wout_d = din("wout", [D, D])
    wg_d = din("wg", [D, FF])
    wu_d = din("wu", [D, FF])
    wd_d = din("wd", [FF, D])
    rsel_d = din("rsel", [128, 2])

    ko = dout("ko", [TT, 256])
    vo = dout("vo", [TT, 256])
    sho = dout("sho", [128, 3, 9])
    wkvo = dout("wkvo", [3, 4, 64, 64])
    yo = dout("yo", [TH, D])
    if dev:
        dbg_mix = nc.dram_tensor("dbg_mix", [4 * 128, TT], BF16, kind="ExternalOutput").ap()

    mixb_in = nc.dram_tensor("mixb_in", [4 * 128, TT], BF16)
    mixb_outs = [nc.dram_tensor(f"mixb_out{c}", [2 * 128, TT], BF16) for c in range(4)]

    P = Prog(nc)
    OUTQ = os.environ.get("K_OUTQ", "sp")
    sink = [None]

    class PH:
        tok = None

    def A(eng, fn, r=(), w=(), **kw):
        if sink[0] is None:
            return P.op(eng, fn, reads=r, writes=w, **kw)
        ph = PH()
        sink[0].append((eng, fn, r, w, kw, ph))
        return ph

    PSUM_KEYS = {"mm0", "mm1", "mm2", "tpb", "X0", "X1", "oT0", "oT1"}

    def safe_boundaries(items):
        n = len(items)
        unsafe = [False] * n
        st_ = {}
        spans = []
        for i, (eng, fn, r, w, kw, ph) in enumerate(items):
            for k in r:
                if k in PSUM_KEYS and k in st_:
                    st_[k][1] = i
                    st_[k][2] = True
            for k in w:
                if k in PSUM_KEYS:
                    if k in st_ and st_[k][2]:
                        spans.append((st_[k][0], st_[k][1]))
                        st_[k] = [i, i, False]
                    elif k not in st_:
                        st_[k] = [i, i, False]
                    else:
                        st_[k][1] = i
        for k, v in st_.items():
            spans.append((v[0], v[1]))
        for a, b in spans:
            for i in range(a, b):
                unsafe[i] = True
        return unsafe

    def replay_items(items, every, mode=None):
        unsafe = safe_boundaries(items)
        since = 0
        nev = 0
        kmin = int(os.environ.get("K_EVMIN", "2"))
        for i, (eng, fn, r, w, kw, ph) in enumerate(items):
            ph.tok = P.op(eng, fn, reads=r, writes=w, **kw)
            since += 1
            if mode == "evac":
                if eng != "pe" and any(k in PSUM_KEYS for k in r):
                    nev += 1
                if nev >= kmin and not unsafe[i]:
                    nev = 0
                    since = 0
                    yield
                elif since >= 4 * every and not unsafe[i]:
                    since = 0
                    yield
            elif since >= every and not unsafe[i]:
                since = 0
                yield

    def interleave(entries):
        act = [[g, max(1, n), 0] for g, n in entries]
        while act:
            g = min(act, key=lambda x: x[2] / x[1])
            try:
                next(g[0])
                g[2] += 1
            except StopIteration:
                act.remove(g)

    out_toks = []

    with contextlib.ExitStack() as st:
        stA = contextlib.ExitStack()
        cur = {"st": st}

        def sb(name, shape, dt=F32):
            return cur["st"].enter_context(nc.sbuf_tensor(name, shape, dt))

        def ps(name, shape, dt=F32):
            return st.enter_context(nc.psum_tensor(name, shape, dt))

        mm = Rot("mm", [ps(f"mm{i}", [128, 512]) for i in range(3)])
        tpb = ps("tpb", [128, 1024], BF16)
        Xb = Rot("X", [ps(f"X{i}", [128, 512]) for i in range(2)])
        oTb = Rot("oT", [ps(f"oT{i}", [128, 512]) for i in range(2)])
        zbk = Rot("X", Xb.bufs)
        aX = Rot("oT", [oTb.bufs[0]])
        aO = Rot("oT1_", [oTb.bufs[1]])

        ident_bf = sb("ident_bf", [128, 128], BF16)
        rsel = sb("rsel_sb", [128, 2])
        cur["st"] = stA
        ones_bf = sb("ones_bf", [128, 512], BF16)
        ident_f = sb("ident_f", [128, 128])
        tri_bf = sb("tri_bf", [128, 128], BF16)
        lst_bf = sb("lst_bf", [128, 128], BF16)
        dmask = sb("dmask", [128, 896], BF16)
        smask_dg = sb("smask_dg", [128, 4, 32], BF16)
        blk1 = sb("blk1", [128, 128])
        blk64 = sb("blk64", [128, 128])
        m_su = sb("m_su", [128, 2, 128], BF16)
        m_sui = sb("m_sui", [128, 2, 128], BF16)
        m_sl = sb("m_sl", [128, 2, 128], BF16)
        scm = sb("scm", [128, 512], BF16)
        scm_s = sb("scm_s", [128, 64])
        pvec = sb("pvec_sb", [128, 24])
        gmix = sb("gmix", [128, D])

        A("pool", lambda e: e.memset(ones_bf[:], 1.0), w=["ones_bf"])

        def asel(out, in_, pat, op, base, cm, rk, wk):
            A("pool", lambda e: e.affine_select(out=out, in_=in_, pattern=pat, compare_op=op,
                                               fill=0.0, base=base, channel_multiplier=cm), r=rk, w=wk)

        asel(ident_bf[:], ones_bf[:, 0:128], [[-1, 128]], ALU.is_equal, 0, 1, ["ones_bf"], ["ident_bf"])
        A("pool", lambda e: e.tensor_copy(out=ident_f[:], in_=ident_bf[:]), r=["ident_bf"], w=["ident_f"])
        asel(tri_bf[:], ones_bf[:, 0:128], [[-1, 128]], ALU.is_ge, 0, 1, ["ones_bf"], ["tri_bf"])
        asel(lst_bf[:], ones_bf[:, 0:128], [[1, 128]], ALU.is_gt, 0, -1, ["ones_bf"], ["lst_bf"])
        asel(dmask[:, 0:512], ones_bf[:], [[1, 512]], ALU.is_gt, -384, -1, ["ones_bf"], ["dmask"])
        asel(dmask[:, 512:896], ones_bf[:, 0:384], [[1, 384]], ALU.is_gt, 128, -1, ["ones_bf"], ["dmask"])
        asel(smask_dg[:], ones_bf[:, 0:128].rearrange("p (h t) -> p h t", t=32), [[0, 4], [1, 32]],
             ALU.is_gt, 0, -1, ["ones_bf"], ["smask_dg"])
        for hh in range(2):
            o2 = ones_bf[:, 0:128]
            asel(m_su[:, hh, :], o2, [[1, 128]], ALU.is_gt, 0, -1, ["ones_bf"], ["m_su"])
            asel(m_sui[:, hh, :], o2, [[1, 128]], ALU.is_ge, 0, -1, ["ones_bf"], ["m_sui"])
            asel(m_sl[:, hh, :], o2, [[-1, 128]], ALU.is_gt, 0, 1, ["ones_bf"], ["m_sl"])
        A("pool", lambda e: e.memset(blk1[:], 0.0), w=["blk1"])
        A("pool", lambda e: e.memset(blk64[:], 0.0), w=["blk64"])
        for hh in range(2):
            sl = slice(hh * 64, hh * 64 + 64)
            A("pool", lambda e, sl=sl: e.memset(blk1[sl, sl], 1.0), w=["blk1"])
            A("pool", lambda e, sl=sl: e.memset(blk64[sl, sl], 1.0 / 64), w=["blk64"])
        A("pool", lambda e: e.memset(scm[:], 1.0), w=["scm"])
        A("pool", lambda e: e.memset(scm[:].rearrange("p (c t) -> p c t", t=128)[:, :, 0:1], 0.0), w=["scm"])
        A("pool", lambda e: e.memset(scm_s[:], 1.0), w=["scm_s"])
        A("pool", lambda e: e.memset(scm_s[:].rearrange("p (c t) -> p c t", t=32)[:, :, 0:1], 0.0), w=["scm_s"])
        A("sp", lambda e: e.dma_start(out=pvec[:], in_=pvec_d), w=["pvec"], dma="c0a")
        A("sp", lambda e: e.dma_start(out=gmix[:], in_=g_mix), w=["gmix"], dma="c0b")
        A("sp", lambda e: e.dma_start(out=rsel[:], in_=rsel_d), w=["rsel"], dma="c0c")

        win_bf = sb("win_bf", [128, 8, NCOL], BF16)
        XPp = sb("XPp", [128, 9, 1, 513])
        xt = Rot("xt", [sb(f"xt{i}", [128, D]) for i in range(2)])
        for kc in range(8):
            A("pool", lambda e, kc=kc: e.dma_start(out=win_bf[:, kc, :], in_=win[kc * 128:(kc + 1) * 128, :]),
              w=["win_bf"], dma="win_bf")
        A("pool", lambda e: e.memset(XPp[:].rearrange("p a b c -> p (a b c)"), 0.0), r=[], w=["XP"])
        wlo_bf = sb("wlo_bf", [128, 256], BF16)
        wgu_bf = sb("wgu_bf", [128, 2, 256], BF16)
        xtt, xk = xt.get()
        A("sp", lambda e, xtt=xtt: e.dma_start(out=xtt[:, 0:256], in_=wlo_d), w=[xk], dma=xk)
        A("sp", lambda e, xtt=xtt: e.dma_start(out=xtt[:, 256:512], in_=wgu_d[0:128, :]), w=[xk], dma=xk)
        A("sp", lambda e, xtt=xtt: e.dma_start(out=xtt[0:32, 512:768], in_=wgu_d[128:160, :]), w=[xk], dma=xk)
        A("dve", lambda e, xtt=xtt: e.tensor_copy(out=wlo_bf[:], in_=xtt[:, 0:256]), r=[xk], w=["wlo_bf"])
        A("dve", lambda e, xtt=xtt: e.tensor_copy(out=wgu_bf[:, 0, :], in_=xtt[:, 256:512]), r=[xk], w=["wgu_bf"])
        A("dve", lambda e, xtt=xtt: e.tensor_copy(out=wgu_bf[0:32, 1, :], in_=xtt[0:32, 512:768]), r=[xk], w=["wgu_bf"])

        stat = Rot("stat", [sb(f"stat{i}", [128, 4]) for i in range(2)])
        hbf = Rot("hbf", [sb(f"hbf{i}", [128, D], BF16) for i in range(2)])
        hT = Rot("hT", [sb(f"hT{i}", [128, 8, 512], BF16) for i in range(1)])
        qT = Rot("qT", [sb(f"qT{i}", [128, 2, 512], BF16) for i in range(2)])
        kT = sb("kT", [128, 2, T], BF16)
        kTn = sb("kTn", [128, 2, 2, 128], BF16)
        vpad = sb("vpad", [128, 32, 4, 128], BF16)
        vnpad = sb("vnpad", [128, 2, 4, 128], BF16)
        kvs = Rot("kvs", [sb(f"kvs{i}", [128, 512]) for i in range(1)])
        XPs = sb("XPs", [128, 9, 2, 33])
        e_t = Rot("e_t", [sb(f"e_t{i}", [128, 512], BF16) for i in range(3)])
        sp_t = Rot("sp_t", [sb(f"sp_t{i}", [128, 512], BF16) for i in range(3)])
        ex_t = Rot("ex_t", [sb(f"ex_t{i}", [128, 512], BF16) for i in range(2)])
        w_t = Rot("w_t", [sb(f"w_t{i}", [128, 512], BF16) for i in range(2)])
        osb = Rot("osb", [sb("osb0", [128, 512], BF16)])
        orw = Rot("orw", [sb("orw0", [128, 512], BF16)])

        A("pool", lambda e: e.memset(vpad[:].rearrange("p a b c -> p (a b c)"), 0.0), w=[f"vpad{i}" for i in range(8)])
        A("pool", lambda e: e.memset(vnpad[:].rearrange("p s b c -> p (s b c)"), 0.0), w=["vnpad"])
        A("pool", lambda e: e.memset(kTn[:].rearrange("p s b c -> p (s b c)"), 0.0), w=["kTn"])
        A("pool", lambda e: e.memset(XPs[:].rearrange("p a b c -> p (a b c)"), 0.0), w=["XPs"])

        c_tiles = [(0, 128), (128, 128), (256, 128), (384, 128)]
        r_tiles = [(768 + 128 * i, 128) for i in range(6)] + [(1536, 128), (1664, 128), (1792, 32)]

        def transpose_tile(hb, hbk, nrows, hTt, hTk, col0, evac_eng):
            for kc in range(8):
                A("pe", lambda e, kc=kc: e.transpose(out=tpb[:, kc * 128:kc * 128 + nrows],
                                                     in_=hb[:nrows, kc * 128:(kc + 1) * 128],
                                                     identity=ident_bf[:nrows, :nrows]),
                  r=[hbk, "ident_bf"], w=["tpb"])
            src = tpb[:].rearrange("p (k t) -> p k t", t=128)[:, :, 0:nrows]
            dst = hTt[:, :, col0:col0 + nrows]
            if evac_eng == "act":
                A("act", lambda e: e.activation(out=dst, in_=src, func=AF.Copy), r=["tpb"], w=[hTk])
            else:
                A("dve", lambda e: e.tensor_copy(out=dst, in_=src), r=["tpb"], w=[hTk])

        def project_block(blk):
            hTt, hTk = hT.get()
            ntok = blk["ntok"]
            for ti, (row0, nrows) in enumerate(blk["tiles"]):
                xtt, xk = xt.get()
                A("sp", lambda e, xtt=xtt, row0=row0, nrows=nrows: e.dma_start(
                    out=xtt[:nrows, :], in_=xa[row0:row0 + nrows, :]), w=[xk], dma=xk)
                hb, hbk = hbf.get()
                stt_r = [xk]
                stt, sk = stat.get()
                A("act", lambda e, xtt=xtt, stt=stt, nrows=nrows, hb=hb: e.activation(
                    out=hb[:nrows, :], in_=xtt[:nrows, :], func=AF.Square, accum_out=stt[:nrows, 0:1]),
                  r=[xk], w=[hbk, sk])
                A("act", lambda e, stt=stt, nrows=nrows: e.activation(
                    out=stt[:nrows, 1:2], in_=stt[:nrows, 0:1], func=AF.Ln, scale=1.0 / D, bias=1e-6),
                  r=[sk], w=[sk])
                A("act", lambda e, stt=stt, nrows=nrows: e.activation(
                    out=stt[:nrows, 2:3], in_=stt[:nrows, 1:2], func=AF.Exp, scale=-0.5), r=[sk], w=[sk])
                A("dve", lambda e, xtt=xtt, stt=stt, hb=hb, nrows=nrows: e.scalar_tensor_tensor(
                    out=hb[:nrows, :], in0=xtt[:nrows, :], scalar=stt[:nrows, 2:3], in1=gmix[:nrows, :],
                    op0=ALU.mult, op1=ALU.mult), r=[xk, sk, "gmix"], w=[hbk])
                transpose_tile(hb, hbk, nrows, hTt, hTk, row0 - blk["tok0"], "act" if ti % 2 == 0 else "dve")
            return hTt, hTk

        def proj_feature_major(blk, hTt, hTk, qTt, qTk, which="all"):
            ntok = blk["ntok"]
            sample = blk["sample"]
            for ci, (c0, wdt) in enumerate(c_tiles if which in ("all", "qk") else []):
                bank, bk = mm.get()
                for kc in range(8):
                    A("pe", lambda e, kc=kc, c0=c0, bank=bank: e.matmul(
                        bank[:, :ntok], lhsT=win_bf[:, kc, c0:c0 + 128], rhs=hTt[:, kc, :ntok],
                        start=(kc == 0), stop=(kc == 7)), r=["win_bf", hTk], w=[bk])
                p = ci % 2
                if ci < 2:
                    A("act", lambda e, bank=bank, p=p: e.activation(out=qTt[:, p, :ntok], in_=bank[:, :ntok],
                                                                     func=AF.Copy), r=[bk], w=[qTk])
                elif not sample:
                    t0 = blk["tok0"]
                    A("dve", lambda e, bank=bank, p=p, t0=t0: e.tensor_copy(out=kT[:, p, t0:t0 + ntok],
                                                                             in_=bank[:, :ntok]),
                      r=[bk], w=[f"kT{t0 // 512}"])
                else:
                    A("dve", lambda e, bank=bank, p=p: e.tensor_copy(
                        out=kTn[:, p, :, 0:NS],
                        in_=bank[:, :ntok].rearrange("p (s t) -> p s t", t=NS)), r=[bk], w=["kTn"])
            for ci, (c0, wdt) in enumerate(r_tiles if which in ("all", "rw") else []):
                bank, bk = mm.get()
                for kc in range(8):
                    A("pe", lambda e, kc=kc, c0=c0, wdt=wdt, bank=bank: e.matmul(
                        bank[:wdt, :ntok], lhsT=win_bf[:, kc, c0:c0 + wdt], rhs=hTt[:, kc, :ntok],
                        start=(kc == 0), stop=(kc == 7)), r=["win_bf", hTk], w=[bk])
                eng = "act" if ci % 4 == 0 else "dve"
                if not sample:
                    dst = XPp[:wdt, ci, 0, 1:513]
                    src = bank[:wdt, :ntok]
                    wk = "XP"
                else:
                    dst = XPs[:wdt, ci, :, 1:33]
                    src = bank[:wdt, :ntok].rearrange("p (s t) -> p s t", t=NS)
                    wk = "XPs"
                if eng == "act":
                    A("act", lambda e, dst=dst, src=src: e.activation(out=dst, in_=src, func=AF.Copy),
                      r=[bk], w=[wk])
                else:
                    A("dve", lambda e, dst=dst, src=src: e.tensor_copy(out=dst, in_=src), r=[bk], w=[wk])

        def proj_token_major(blk, hTt, hTk):
            for ti, (row0, nrows) in enumerate(blk["tiles"]):
                col0 = row0 - blk["tok0"]
                bank, bk = mm.get()
                for kc in range(8):
                    A("pe", lambda e, kc=kc, bank=bank, col0=col0, nrows=nrows: e.matmul(
                        bank[:nrows, :], lhsT=hTt[:, kc, col0:col0 + nrows], rhs=win_bf[:, kc, 256:768],
                        start=(kc == 0), stop=(kc == 7)), r=["win_bf", hTk], w=[bk])
                kv, kvk = kvs.get()
                A("act", lambda e, kv=kv, bank=bank, nrows=nrows: e.activation(
                    out=kv[:nrows, :], in_=bank[:nrows, :], func=AF.Copy), r=[bk], w=[kvk])
                SKIP = int(os.environ.get("K_SKIP", "0"))
                if not (SKIP & 1):
                  out_toks.append(A(OUTQ, lambda e, kv=kv, row0=row0, nrows=nrows: e.dma_start(
                    out=ko[row0:row0 + nrows, :], in_=kv[:nrows, 0:256]), r=[kvk], dma="o" + kvk))
                if not (SKIP & 1):
                  out_toks.append(A(OUTQ, lambda e, kv=kv, row0=row0, nrows=nrows: e.dma_start(
                    out=vo[row0:row0 + nrows, :], in_=kv[:nrows, 256:512]), r=[kvk], dma="o" + kvk))
                src = kv[:nrows, 256:512].rearrange("p (a b d) -> p a b d", a=2, b=2)
                if not blk["sample"]:
                    kt = row0 // 128
                    dstf = vpad[:nrows, kt, :, :].rearrange("p h c -> p (h c)")
                    wk = f"vpad{row0 // 512}"
                else:
                    sq = ti
                    dstf = vnpad[:nrows, sq, :, :].rearrange("p h c -> p (h c)")
                    wk = "vnpad"
                for b_ in range(2):
                    dst = dstf[:, b_ * 192: b_ * 192 + 64 + 256].rearrange("p (a r) -> p a r", r=320) \
                        if False else None
                for a_ in range(2):
                    for b_ in range(2):
                        if SKIP & 2:
                            continue
                        off = a_ * 256 + b_ * 192
                        A("dve", lambda e, off=off, a_=a_, b_=b_, dstf=dstf, src=src: e.tensor_copy(
                            out=dstf[:, off:off + 64], in_=src[:, a_, b_, :]), r=[kvk], w=[wk])

        def attn_unit(ncols, groups, kblocks, zsingle=False):
            X, Xk = aX.bufs[0], "oT0"
            nb = len(kblocks)
            st_ = {}

            def stage_a(bi):
                kb = kblocks[bi]
                nk = kb["nk"]
                zb, zk = zbk.get()
                if zsingle:
                    zb, zk = zbk.bufs[0], "X0"
                for g in groups:
                    A("pe", lambda e, g=g, kb=kb, zb=zb, nk=nk: e.matmul(
                        zb[:nk, g["c0"]:g["c0"] + g["nc"]], lhsT=g["kT"](kb), rhs=g["q"], start=True, stop=True),
                      r=g["rk"] + ([f"kT{kb['kt'] // 4}"] if kb["kt"] >= 0 else []), w=[zk])
                et, ek = e_t.get()
                A("act", lambda e, et=et, zb=zb, nk=nk: e.activation(out=et[:nk, :ncols], in_=zb[:nk, :ncols],
                                                                      func=AF.Exp, scale=0.125), r=[zk], w=[ek])
                spt, spk = sp_t.get()
                A("act", lambda e, et=et, spt=spt, nk=nk: e.activation(out=spt[:nk, :ncols], in_=et[:nk, :ncols],
                                                                        func=AF.Ln, bias=1.0), r=[ek], w=[spk])
                if kb["mask"] is not None:
                    mk = kb["mask"]
                    A("dve", lambda e, spt=spt, mk=mk, nk=nk: e.tensor_tensor(
                        out=spt[:nk, :ncols], in0=spt[:nk, :ncols], in1=mk, op=ALU.mult), r=[spk, "dmask"], w=[spk])
                    A("dve", lambda e, et=et, mk=mk, nk=nk: e.tensor_tensor(
                        out=et[:nk, :ncols], in0=et[:nk, :ncols], in1=mk, op=ALU.mult), r=[ek, "dmask"], w=[ek])
                st_[bi] = (et, ek, spt, spk)

            def pv(bi):
                kb = kblocks[bi]
                nk = kb["nk"]
                wt, wk = st_[("w", bi)]
                for g in groups:
                    A("pe", lambda e, g=g, kb=kb, wt=wt, nk=nk, bi=bi: e.matmul(
                        g["out"], lhsT=g["v"](kb), rhs=wt[:nk, g["c0"]:g["c0"] + g["nc"]],
                        start=(g["first"] and bi == 0), stop=(g["last"] and bi == nb - 1),
                        skip_group_check=True),
                      r=[wk] + g["rv"] + ([f"vpad{kb['kt'] // 4}"] if kb["kt"] >= 0 else []), w=[g["ok"]])

            stage_a(0)
            if nb > 1:
                stage_a(1)
            for bi, kb in enumerate(kblocks):
                nk = kb["nk"]
                et, ek, spt, spk = st_.pop(bi)
                A("pe", lambda e, X=X, spt=spt, nk=nk, bi=bi: e.matmul(
                    X[:, :ncols], lhsT=tri_bf[:nk, :], rhs=spt[:nk, :ncols], start=(bi == 0), stop=False,
                    skip_group_check=True), r=[spk, "tri_bf"], w=[Xk])
                ext, exk = ex_t.get()
                A("act", lambda e, ext=ext, X=X, nk=nk: e.activation(out=ext[:nk, :ncols], in_=X[:nk, :ncols],
                                                                      func=AF.Exp, scale=-1.0), r=[Xk], w=[exk])
                if bi + 2 < nb:
                    stage_a(bi + 2)
                if bi > 0:
                    pv(bi - 1)
                if bi < nb - 1:
                    A("pe", lambda e, X=X, spt=spt, nk=nk, bi=bi: e.matmul(
                        X[:, :ncols], lhsT=lst_bf[:nk, :], rhs=spt[:nk, :ncols], start=False, stop=(bi == nb - 2),
                        skip_group_check=True), r=[spk, "lst_bf"], w=[Xk])
                wt, wk = w_t.get()
                A("dve", lambda e, wt=wt, et=et, ext=ext, nk=nk: e.tensor_tensor(
                    out=wt[:nk, :ncols], in0=et[:nk, :ncols], in1=ext[:nk, :ncols], op=ALU.mult),
                  r=[ek, exk], w=[wk])
                st_[("w", bi)] = (wt, wk)
                yield
            pv(nb - 1)

        def mix_store(src_ap, srck, chunk, col0, ncol):
            return A(OUTQ, lambda e: e.dma_start(out=mixb_in[chunk * 128:(chunk + 1) * 128, col0:col0 + ncol],
                                                   in_=src_ap), r=[srck], w=["mixb_in"], dma="m" + srck)

        def attention_prompt(qb, qTt, qTk):
            for p in range(2):
                oT, oTk = aO.bufs[0], "oT1"
                for e_ in range(2):
                    h = 2 * p + e_
                    sl = slice(e_ * 64, e_ * 64 + 64)
                    kbs = []
                    for kt in range(4 * qb + 3, -1, -1):
                        i = kt - 4 * qb
                        kbs.append(dict(nk=128, kt=kt, mask=(dmask[:, 384 - 128 * i:896 - 128 * i] if i >= 0 else None)))
                    g = dict(c0=0, nc=512, q=qTt[sl, p, :], rk=[qTk], rv=[],
                             kT=lambda kb, sl=sl, p=p: kT[sl, p, kb["kt"] * 128:(kb["kt"] + 1) * 128],
                             v=lambda kb, h=h: vpad[:, kb["kt"], h, :],
                             out=oT[:, :], ok=oTk, first=(e_ == 0), last=(e_ == 1))
                    yield from attn_unit(512, [g], kbs)
                ob, obk = osb.get()
                A("act", lambda e, ob=ob, oT=oT: e.activation(out=ob[:, :], in_=oT[:, :], func=AF.Copy),
                  r=[oTk], w=[obk])
                mix_store(ob[:, :], obk, p, qb * 512, 512)

        def attention_sample(qTt, qTk):
            SAMP = int(os.environ.get("K_SAMP", "7"))
            for sq in range(2):
                if SAMP & 1:
                    load_cache(sq)
                if not (SAMP & 6):
                    continue
                oTs = [(aO.bufs[0], "oT1"), (zbk.bufs[1], "X1")]
                kbs = [dict(nk=128, kt=-1, mask=smask_dg[:].rearrange("p h t -> p (h t)"))]
                for kt in range(31, 15, -1):
                    if SAMP & 4:
                        kbs.append(dict(nk=128, kt=kt, mask=None))
                msk = smask_dg[:, 0:2, :].rearrange("p h t -> p (h t)")
                for kb in kbs:
                    if kb["mask"] is not None:
                        kb["mask"] = msk
                for e_ in range(2):
                    sl = slice(e_ * 64, e_ * 64 + 64)
                    groups = []
                    for p in range(2):
                        h = 2 * p + e_
                        groups.append(dict(
                            c0=p * NS, nc=NS, q=qTt[sl, p, sq * NS:(sq + 1) * NS], rk=[qTk, "kTn"],
                            rv=["vnpad"],
                            kT=lambda kb, sl=sl, p=p, sq=sq: (kTn[sl, p, sq, :] if kb["kt"] < 0 else
                                                              kT[sl, p, kb["kt"] * 128:(kb["kt"] + 1) * 128]),
                            v=lambda kb, h=h, sq=sq: (vnpad[:, sq, h, :] if kb["kt"] < 0 else vpad[:, kb["kt"], h, :]),
                            out=oTs[p][0][:, 0:NS], ok=oTs[p][1], first=(e_ == 0), last=(e_ == 1)))
                    yield from attn_unit(2 * NS, groups, kbs, zsingle=True)
                for p in range(2):
                    ob, obk = osb.get()
                    A("act", lambda e, ob=ob, p=p: e.activation(out=ob[:, 0:NS], in_=oTs[p][0][:, 0:NS],
                                                                 func=AF.Copy), r=[oTs[p][1]], w=[obk])
                    mix_store(ob[:, 0:NS], obk, p, T + sq * NS, NS)

        def load_cache(sq):
            for kt in range(16, 32):
                xtt, xk = xt.get()
                A("sp", lambda e, xtt=xtt, kt=kt: e.dma_start(
                    out=xtt[:, 0:256], in_=ck_d[sq, (kt - 16) * 128:(kt - 15) * 128, :]), w=[xk], dma=xk)
                A("sp", lambda e, xtt=xtt, kt=kt: e.dma_start(
                    out=xtt[:, 256:512], in_=cv_d[sq, (kt - 16) * 128:(kt - 15) * 128, :]), w=[xk], dma=xk)
                hb, hbk = hbf.get()
                A("dve", lambda e, hb=hb, xtt=xtt: e.tensor_copy(out=hb[:, 0:256], in_=xtt[:, 0:256]),
                  r=[xk], w=[hbk])
                for p in range(2):
                    A("pe", lambda e, hb=hb, p=p: e.transpose(out=tpb[:, p * 128:(p + 1) * 128],
                                                              in_=hb[:, p * 128:(p + 1) * 128],
                                                              identity=ident_bf[:]),
                      r=[hbk, "ident_bf"], w=["tpb"])
                A("act", lambda e, kt=kt: e.activation(
                    out=kT[:, :, kt * 128:(kt + 1) * 128],
                    in_=tpb[:, 0:256].rearrange("p (a t) -> p a t", t=128), func=AF.Copy),
                  r=["tpb"], w=[f"kT{kt // 4}"])
                dstf = vpad[:, kt, :, :].rearrange("p h c -> p (h c)")
                for a_ in range(2):
                    for b_ in range(2):
                        off = a_ * 256 + b_ * 192
                        hidx = 2 * a_ + b_
                        A("pool", lambda e, off=off, hidx=hidx, dstf=dstf, xtt=xtt: e.tensor_copy(
                            out=dstf[:, off:off + 64], in_=xtt[:, 256 + hidx * 64:256 + hidx * 64 + 64]),
                          r=[xk], w=[f"vpad{kt // 4}"])


        SH = Rot("SH", [sb(f"SH{i}", [128, 512]) for i in range(1)])
        lo_bf = sb("lo_bf", [128, 512], BF16)
        sg_bf = sb("sg_bf", [128, 2, 512], BF16)
        SWt = sb("SWt", [128, 2, 512])
        At = sb("At", [128, 2, 512])
        Gt = sb("Gt", [128, 2, 512], BF16)
        Fs = [sb(f"F{i}", [128, 512]) for i in range(8)]
        Hs = [sb(f"H{i}", [128, 512], BF16) for i in range(7)]
        VZ = sb("VZ", [128, 4, 192], BF16)
        BKt = sb("BKt", [128, 4, 256], BF16)
        M4 = [sb(f"M4_{e_}", [128, 4, 128], BF16) for e_ in range(2)]
        RKm = sb("RKm", [128, 2, 128], BF16)
        XX = [sb(f"XX{i}", [128, 4, 128], BF16) for i in range(2)]
        PP = [sb(f"PP{i}", [128, 2, 128], BF16) for i in range(2)]
        mk4 = sb("mk4", [128, 4, 128], BF16)
        S_f = [sb(f"S_f{p}", [128, 128]) for p in range(4)]
        S_bf = [sb(f"S_bf{p}", [128, 128], BF16) for p in range(4)]
        W1b = sb("W1b", [128, 128], BF16)
        UZ = sb("UZ", [128, 192], BF16)
        gct = sb("gct", [128, 8])
        pv2 = sb("pv2", [128, 2])
        sst = sb("sst", [128, 2, 9])
        SW0 = sb("SW0", [64, 2, 64])
        Sout = sb("Sout", [128, 64])
        id2 = sb("id2", [128, 2, 128], BF16)

        A("pool", lambda e: e.memset(VZ[:].rearrange("p a b -> p (a b)"), 0.0), w=["VZ"])
        A("pool", lambda e: e.memset(UZ[:], 0.0), w=["UZ"])
        for i_, mk_ in enumerate([m_su, m_sl, m_su, m_sui]):
            A("pool", lambda e, i_=i_, mk_=mk_: e.tensor_copy(out=mk4[:, i_, :], in_=mk_[:, 0, :]),
              r=["m_su", "m_sl", "m_sui"], w=["mk4"])
        for e_ in range(2):
            A("pool", lambda e, e_=e_: e.tensor_copy(out=id2[:, e_, :], in_=ident_bf[:]), r=["ident_bf"], w=["id2"])
        for p in range(2):
            A("dve", lambda e, p=p: e.tensor_scalar(out=pv2[:, p:p + 1], in0=pvec[:, 9 + p * 7 + 3:9 + p * 7 + 4],
                                                    scalar1=-1.0, scalar2=1.0, op0=ALU.mult, op1=ALU.add),
              r=["pvec"], w=["pv2"])
        A("sp", lambda e: e.dma_start(out=sst[:], in_=sshift_d), w=["sst"], dma="c2a")
        for sq in range(2):
            A("pool", lambda e, sq=sq: e.tensor_copy(out=XPs[:, :, sq, 0], in_=sst[:, sq, :]), r=["sst"], w=["XPs"])

        def state_zero(si):
            A("pool", lambda e: e.memset(S_f[si][:], 0.0), w=[f"S_f{si}"])
            A("pool", lambda e: e.memset(S_bf[si][:], 0.0), w=[f"S_bf{si}"])

        def state_load(si, p, sq):
            state_zero(si)
            A("sp", lambda e: e.dma_start(out=SW0[:], in_=swkv_d[sq, 2 * p:2 * p + 2].rearrange("e v k -> v e k")),
              w=["SW0"], dma="sw0")
            bank, bk = mm.get()
            A("pe", lambda e: e.transpose(out=bank[:, 0:64], in_=SW0[:].rearrange("v e k -> v (e k)"),
                                          identity=ident_f[0:64, 0:64]), r=["SW0", "ident_f"], w=[bk])
            for e_ in range(2):
                sl = slice(e_ * 64, e_ * 64 + 64)
                A("dve", lambda e, sl=sl: e.tensor_copy(out=S_f[si][sl, sl], in_=bank[sl, 0:64]), r=[bk], w=[f"S_f{si}"])
            for e_ in range(2):
                sl = slice(e_ * 64, e_ * 64 + 64)
                A("pool", lambda e, sl=sl: e.tensor_copy(out=S_bf[si][sl, sl], in_=S_f[si][sl, sl]),
                  r=[f"S_f{si}"], w=[f"S_bf{si}"])

        def state_store(si, p, which):
            bank, bk = mm.get()
            A("pe", lambda e: e.transpose(out=bank[:, 0:128], in_=S_f[si][:], identity=ident_f[:]),
              r=[f"S_f{si}", "ident_f"], w=[bk])
            for e_ in range(2):
                sl = slice(e_ * 64, e_ * 64 + 64)
                A("dve", lambda e, sl=sl: e.tensor_copy(out=Sout[sl, :], in_=bank[sl, sl]), r=[bk], w=["Sout"])
            out_toks.append(A(OUTQ, lambda e: e.dma_start(
                out=wkvo[which, 2 * p:2 * p + 2].rearrange("e v k -> (e v) k"), in_=Sout[:]),
                r=["Sout"], dma="osout"))

        def rwkv_block(blk):
            sample = blk["sample"]
            if sample:
                XP, XPk, nseg, L, C = XPs, "XPs", 2, NS, NS
                scmask = scm_s
            else:
                XP, XPk, nseg, L, C = XPp, "XP", 1, 512, 128
                scmask = scm
            ntok = nseg * L
            nch = ntok // C
            nlev = int(round(math.log2(C)))
            cur = lambda ct, rows=128: XP[:rows, ct, :, 1:L + 1]
            prv = lambda ct, rows=128: XP[:rows, ct, :, 0:L]
            v3 = lambda ap, rows=128: ap[:rows, :ntok].rearrange("p (s l) -> p s l", l=L)

            for ct in range(9):
                rows = 32 if ct == 8 else 128
                sh, shk = SH.get()
                A("dve", lambda e, ct=ct, rows=rows, sh=sh: e.tensor_tensor(
                    out=v3(sh, rows), in0=prv(ct, rows), in1=cur(ct, rows), op=ALU.subtract), r=[XPk], w=[shk])
                A("pool", lambda e, ct=ct, rows=rows: e.tensor_copy(out=XP[:rows, ct, :, 0:1],
                                                                     in_=XP[:rows, ct, :, L:L + 1]),
                  r=[XPk, shk], w=[XPk])
                A("dve", lambda e, ct=ct, rows=rows, sh=sh: e.scalar_tensor_tensor(
                    out=cur(ct, rows), in0=v3(sh, rows), scalar=pvec[:rows, ct:ct + 1], in1=cur(ct, rows),
                    op0=ALU.mult, op1=ALU.add), r=[shk, XPk, "pvec"], w=[XPk])

            A("act", lambda e: e.activation(out=v3(lo_bf, 64), in_=cur(6, 64), func=AF.Tanh), r=[XPk], w=["lo_bf"])
            A("dve", lambda e: e.tensor_copy(out=v3(lo_bf)[64:128], in_=cur(6)[64:128]), r=[XPk], w=["lo_bf"])
            A("act", lambda e: e.activation(out=v3(sg_bf[:, 0, :]), in_=cur(7), func=AF.Sigmoid), r=[XPk], w=["sg_bf"])
            A("act", lambda e: e.activation(out=v3(sg_bf[:, 1, :], 32), in_=cur(8, 32), func=AF.Sigmoid),
              r=[XPk], w=["sg_bf"])
            for p in range(2):
                pc = lambda i, p=p: pvec[:, 9 + p * 7 + i:9 + p * 7 + i + 1]
                cs = slice(p * 128, (p + 1) * 128)
                bank, bk = mm.get()
                A("pe", lambda e, bank=bank, cs=cs: e.matmul(bank[:, :ntok], lhsT=wlo_bf[0:64, cs], rhs=lo_bf[0:64, :ntok],
                                                             start=True, stop=True), r=["wlo_bf", "lo_bf"], w=[bk])
                A("act", lambda e, bank=bank, p=p, pc=pc: e.activation(out=SWt[:, p, :ntok], in_=bank[:, :ntok],
                                                                      func=AF.Sigmoid, bias=pc(0)), r=[bk, "pvec"], w=["SWt"])
                bank, bk = mm.get()
                A("pe", lambda e, bank=bank, cs=cs: e.matmul(bank[:, :ntok], lhsT=wlo_bf[64:128, cs], rhs=lo_bf[64:128, :ntok],
                                                             start=True, stop=True), r=["wlo_bf", "lo_bf"], w=[bk])
                A("act", lambda e, bank=bank, p=p, pc=pc: e.activation(out=At[:, p, :ntok], in_=bank[:, :ntok],
                                                                      func=AF.Sigmoid, bias=pc(1)), r=[bk, "pvec"], w=["At"])
                bank, bk = mm.get()
                A("pe", lambda e, bank=bank, cs=cs: e.matmul(bank[:, :ntok], lhsT=wgu_bf[:, 0, cs], rhs=sg_bf[:, 0, :ntok],
                                                             start=True, stop=False), r=["wgu_bf", "sg_bf"], w=[bk])
                A("pe", lambda e, bank=bank, cs=cs: e.matmul(bank[:, :ntok], lhsT=wgu_bf[0:32, 1, cs], rhs=sg_bf[0:32, 1, :ntok],
                                                             start=False, stop=True), r=["wgu_bf", "sg_bf"], w=[bk])
                A("dve", lambda e, bank=bank, p=p: e.tensor_copy(out=Gt[:, p, :ntok], in_=bank[:, :ntok]), r=[bk], w=["Gt"])

            for p in range(2):
                rwkv_pair(blk, p, XP, XPk, nseg, L, C, ntok, nch, nlev, scmask, cur, v3)

        def rwkv_pair(blk, p, XP, XPk, nseg, L, C, ntok, nch, nlev, scmask, cur, v3):
            sample = blk["sample"]
            pc = lambda i: pvec[:, 9 + p * 7 + i:9 + p * 7 + i + 1]
            R, KR, VR = cur(p), cur(2 + p), cur(4 + p)
            F = lambda i: Fs[i][:, :ntok]
            Hh = lambda i: Hs[i][:, :ntok]
            Fk = lambda i: f"F{i}"
            Hk = lambda i: f"H{i}"
            SWp, Ap = SWt[:, p, :ntok], At[:, p, :ntok]
            A("dve", lambda e: e.tensor_scalar(out=v3(Fs[0]), in0=KR, scalar1=pc(2), scalar2=None, op0=ALU.mult),
              r=[XPk, "pvec"], w=[Fk(0)])
            A("dve", lambda e: e.tensor_tensor(out=F(1), in0=F(0), in1=F(0), op=ALU.mult), r=[Fk(0)], w=[Fk(1)])
            bank, bk = mm.get()
            A("pe", lambda e, bank=bank: e.matmul(bank[:, :ntok], lhsT=blk1[:], rhs=F(1), start=True, stop=True),
              r=["blk1", Fk(1)], w=[bk])
            A("dve", lambda e, bank=bank: e.tensor_scalar(out=F(1), in0=bank[:, :ntok], scalar1=1e-24, scalar2=None,
                                                          op0=ALU.max), r=[bk], w=[Fk(1)])
            A("act", lambda e: e.activation(out=F(1), in_=F(1), func=AF.Ln), r=[Fk(1)], w=[Fk(1)])
            A("act", lambda e: e.activation(out=F(1), in_=F(1), func=AF.Exp, scale=-0.5), r=[Fk(1)], w=[Fk(1)])
            A("dve", lambda e: e.tensor_tensor(out=F(0), in0=F(0), in1=F(1), op=ALU.mult), r=[Fk(0), Fk(1)], w=[Fk(0)])
            A("dve", lambda e: e.tensor_tensor(out=F(2), in0=F(0), in1=Ap, op=ALU.mult), r=[Fk(0), "At"], w=[Fk(2)])
            A("dve", lambda e: e.tensor_scalar(out=F(3), in0=Ap, scalar1=pc(3), scalar2=pv2[:, p:p + 1],
                                               op0=ALU.mult, op1=ALU.add), r=["At", "pvec", "pv2"], w=[Fk(3)])
            A("dve", lambda e: e.tensor_tensor(out=v3(Fs[3]), in0=v3(Fs[3]), in1=KR, op=ALU.mult), r=[Fk(3), XPk], w=[Fk(3)])
            A("dve", lambda e: e.tensor_tensor_scan(out=F(4), data0=scmask[:, :ntok], data1=SWp, initial=0.0,
                                                    op0=ALU.mult, op1=ALU.add), r=["scm", "SWt"], w=[Fk(4)])
            A("act", lambda e: e.activation(out=F(5), in_=F(4), func=AF.Exp, scale=-DEC), r=[Fk(4)], w=[Fk(5)])
            A("dve", lambda e: e.tensor_tensor(out=v3(Hs[0]), in0=R, in1=v3(Fs[5]), op=ALU.mult), r=[XPk, Fk(5)], w=[Hk(0)])
            A("act", lambda e: e.activation(out=F(5), in_=F(4), func=AF.Exp, scale=DEC), r=[Fk(4), Hk(0)], w=[Fk(5)])
            A("dve", lambda e: e.tensor_tensor(out=Hh(1), in0=F(2), in1=F(5), op=ALU.mult), r=[Fk(2), Fk(5)], w=[Hk(1)])
            A("dve", lambda e: e.tensor_tensor(out=Hh(2), in0=F(3), in1=F(5), op=ALU.mult), r=[Fk(3), Fk(5)], w=[Hk(2)])
            A("dve", lambda e: e.tensor_tensor(out=F(6), in0=F(4), in1=SWp, op=ALU.subtract), r=[Fk(4), "SWt"], w=[Fk(6)])
            A("act", lambda e: e.activation(out=F(6), in_=F(6), func=AF.Exp, scale=-DEC), r=[Fk(6)], w=[Fk(6)])
            A("dve", lambda e: e.scalar_tensor_tensor(out=Hh(3), in0=F(0), scalar=-1.0, in1=F(6), op0=ALU.mult, op1=ALU.mult),
              r=[Fk(0), Fk(6)], w=[Hk(3)])
            for ci in range(nch):
                cc = slice(ci * C, (ci + 1) * C)
                A("dve", lambda e, cc=cc, ci=ci: e.tensor_scalar(out=Fs[7][:, cc], in0=Fs[4][:, cc],
                                                                 scalar1=Fs[4][:, (ci + 1) * C - 1:(ci + 1) * C],
                                                                 scalar2=None, op0=ALU.subtract), r=[Fk(4)], w=[Fk(7)])
            A("act", lambda e: e.activation(out=F(7), in_=F(7), func=AF.Exp, scale=DEC), r=[Fk(7)], w=[Fk(7)])
            A("dve", lambda e: e.tensor_tensor(out=Hh(4), in0=F(2), in1=F(7), op=ALU.mult), r=[Fk(2), Fk(7)], w=[Hk(4)])
            A("dve", lambda e: e.tensor_tensor(out=Hh(5), in0=F(3), in1=F(7), op=ALU.mult), r=[Fk(3), Fk(7)], w=[Hk(5)])
            A("act", lambda e: e.activation(out=gct[:, 0:nch], in_=Fs[4][:, C - 1:ntok:C], func=AF.Exp, scale=-DEC),
              r=[Fk(4)], w=["gct"])
            A("act", lambda e: e.activation(out=v3(Hs[6]), in_=VR, func=AF.Copy), r=[XPk], w=[Hk(6)])
            for ci in range(nch):
                cc = slice(ci * C, (ci + 1) * C)
                for i_, hi in enumerate((6, 4, 5)):
                    A("pe", lambda e, cc=cc, i_=i_, hi=hi: e.transpose(out=tpb[:C, i_ * 128:(i_ + 1) * 128],
                                                                       in_=Hs[hi][:, cc], identity=ident_bf[:]),
                      r=[Hk(hi), "ident_bf"], w=["tpb"])
                A("dve", lambda e, ci=ci: e.tensor_copy(
                    out=VZ[:C, ci, :].rearrange("p (b d) -> p b d", d=64)[:, 0::2, :],
                    in_=tpb[:C, 0:128].rearrange("p (b d) -> p b d", d=64)), r=["tpb"], w=["VZ"])
                A("dve", lambda e, ci=ci: e.tensor_copy(out=BKt[:C, ci, :], in_=tpb[:C, 128:384]),
                  r=["tpb"], w=["BKt"])

            YT = 1
            for ci in range(nch):
                cc = slice(ci * C, (ci + 1) * C)
                si = p + 2 if sample else p
                if sample:
                    state_load(si, p, ci)
                elif blk["tok0"] == 0 and ci == 0:
                    state_zero(si)
                rkb = []
                for e_ in range(2):
                    sl = slice(e_ * 64, e_ * 64 + 64)
                    bank, bk = mm.get()
                    specs = [(1, 3), (3, 1), (2, 3), (1, 0)]
                    for i_, (li, ri) in enumerate(specs):
                        A("pe", lambda e, bank=bank, i_=i_, li=li, ri=ri, sl=sl, cc=cc: e.matmul(
                            bank[:C, i_ * 128:i_ * 128 + C], lhsT=Hs[li][sl, cc], rhs=Hs[ri][sl, cc],
                            start=True, stop=True), r=[Hk(li), Hk(ri)], w=[bk])
                    A("dve", lambda e, bank=bank, e_=e_: e.tensor_tensor(
                        out=M4[e_][:C, :, :C], in0=bank[:C, :].rearrange("p (a b) -> p a b", b=128)[:, :, :C],
                        in1=mk4[:C, :, :C], op=ALU.mult), r=[bk, "mk4"], w=[f"M4{e_}"])
                    bank2, bk2 = mm.get()
                    A("pe", lambda e, bank2=bank2, sl=sl, cc=cc: e.matmul(
                        bank2[:C, 0:C], lhsT=Hs[2][sl, cc], rhs=Hs[0][sl, cc], start=True, stop=True),
                      r=[Hk(2), Hk(0)], w=[bk2])
                    A("dve", lambda e, bank2=bank2, e_=e_: e.tensor_tensor(
                        out=RKm[:C, e_, :C], in0=bank2[:C, 0:C], in1=m_sui[:C, 0, :C], op=ALU.mult),
                      r=[bk2, "m_sui"], w=["RKm"])
                Xc = lambda e_: M4[e_][:C, 0, :C]
                XTc = lambda e_: M4[e_][:C, 1, :C]
                xkeys = ["M40", "M41"]
                A("dve", lambda e: e.tensor_tensor(out=PP[0][:C, 0, :C], in0=M4[0][:C, 0, :C], in1=ident_bf[:C, :C],
                                                   op=ALU.add), r=["M40", "ident_bf"], w=["PP0"])
                A("dve", lambda e: e.tensor_tensor(out=PP[0][:C, 1, :C], in0=M4[1][:C, 0, :C], in1=ident_bf[:C, :C],
                                                   op=ALU.add), r=["M41", "ident_bf"], w=["PP0"])
                pi = 0
                for k in range(1, nlev):
                    need_x = k <= nlev - 2
                    xi = k % 2
                    bank, bk = mm.get()
                    for e_ in range(2):
                        A("pe", lambda e, bank=bank, e_=e_, Xc=Xc, XTc=XTc: e.matmul(
                            bank[:C, (2 + e_) * 128:(2 + e_) * 128 + C], lhsT=Xc(e_), rhs=XTc(e_), start=True, stop=True),
                          r=xkeys, w=[bk])
                        if need_x:
                            A("pe", lambda e, bank=bank, e_=e_, Xc=Xc, XTc=XTc: e.matmul(
                                bank[:C, e_ * 128:e_ * 128 + C], lhsT=XTc(e_), rhs=Xc(e_), start=True, stop=True),
                              r=xkeys, w=[bk])
                    lo_ = 0 if need_x else 2
                    A("dve", lambda e, bank=bank, xi=xi, lo_=lo_: e.tensor_copy(
                        out=XX[xi][:C, lo_:4, :C],
                        in_=bank[:C, :].rearrange("p (a b) -> p a b", b=128)[:, lo_:4, :C]),
                      r=[bk], w=[f"XX{xi}"])
                    Xc = lambda e_, xi=xi: XX[xi][:C, e_, :C]
                    XTc = lambda e_, xi=xi: XX[xi][:C, 2 + e_, :C]
                    xkeys = [f"XX{xi}"]
                    bank, bk = mm.get()
                    for e_ in range(2):
                        A("pe", lambda e, bank=bank, e_=e_, XTc=XTc, pi=pi: e.matmul(
                            bank[:C, e_ * 128:e_ * 128 + C], lhsT=XTc(e_), rhs=PP[pi][:C, e_, :C], start=True, stop=True),
                          r=xkeys + [f"PP{pi}"], w=[bk])
                    A("dve", lambda e, bank=bank, pi=pi: e.tensor_tensor(
                        out=PP[1 - pi][:C, :, :C], in0=bank[:C, 0:256].rearrange("p (a b) -> p a b", b=128)[:, :, :C],
                        in1=PP[pi][:C, :, :C], op=ALU.add), r=[bk, f"PP{pi}"], w=[f"PP{1 - pi}"])
                    pi = 1 - pi
                Sb, Sbk, Sf, Sfk = S_bf[si], f"S_bf{si}", S_f[si], f"S_f{si}"
                bank, bk = mm.get()
                A("pe", lambda e, bank=bank, cc=cc: e.matmul(bank[:C, 0:128], lhsT=Hs[3][:, cc], rhs=Sb[:], start=True, stop=False,
                                                             skip_group_check=True), r=[Hk(3), Sbk], w=[bk])
                for e_ in range(2):
                    A("pe", lambda e, bank=bank, e_=e_, ci=ci: e.matmul(
                        bank[:C, e_ * 64:(e_ + 1) * 64], lhsT=M4[e_][:C, 2, :C], rhs=VZ[:C, ci, e_ * 128:e_ * 128 + 64],
                        start=False, stop=(e_ == 1), skip_group_check=True), r=[f"M4{e_}", "VZ"], w=[bk])
                A("dve", lambda e, bank=bank: e.tensor_copy(out=W1b[:C, :], in_=bank[:C, 0:128]), r=[bk], w=["W1b"])
                bank, bk = mm.get()
                for e_ in range(2):
                    A("pe", lambda e, bank=bank, e_=e_, pi=pi: e.matmul(
                        bank[:C, e_ * 64:(e_ + 1) * 64], lhsT=PP[pi][:C, e_, :C], rhs=W1b[:C, e_ * 64:(e_ + 1) * 64],
                        start=True, stop=True), r=[f"PP{pi}", "W1b"], w=[bk])
                A("dve", lambda e, bank=bank: e.tensor_copy(
                    out=UZ[:C, :].rearrange("p (b d) -> p b d", d=64)[:, 0::2, :],
                    in_=bank[:C, 0:128].rearrange("p (b d) -> p b d", d=64)), r=[bk], w=["UZ"])
                bank, bk = mm.get()
                A("pe", lambda e, bank=bank, cc=cc: e.matmul(bank[:, 0:C], lhsT=Sb[:], rhs=Hs[0][:, cc], start=True, stop=False,
                                                             skip_group_check=True), r=[Sbk, Hk(0)], w=[bk])
                for e_ in range(2):
                    A("pe", lambda e, bank=bank, e_=e_: e.matmul(
                        bank[:, 0:C], lhsT=UZ[:C, e_ * 64:e_ * 64 + 128], rhs=M4[e_][:C, 3, :C], start=False, stop=False,
                        skip_group_check=True), r=["UZ", f"M4{e_}"], w=[bk])
                    A("pe", lambda e, bank=bank, e_=e_, ci=ci: e.matmul(
                        bank[:, 0:C], lhsT=VZ[:C, ci, e_ * 64:e_ * 64 + 128], rhs=RKm[:C, e_, :C], start=False,
                        stop=(e_ == 1), skip_group_check=True), r=["VZ", "RKm"], w=[bk])
                A("dve", lambda e, bank=bank, cc=cc: e.tensor_copy(out=Fs[YT][:, cc], in_=bank[:, 0:C]),
                  r=[bk], w=[Fk(YT)])
                bank, bk = mm.get()
                A("pe", lambda e, bank=bank, ci=ci: e.matmul(
                    bank[:, 0:128], lhsT=BKt[:C, ci, 0:128], rhs=UZ[:C, :].rearrange("p (b d) -> p b d", d=64)[:, 0::2, :],
                    start=True, stop=False, skip_group_check=True), r=["BKt", "UZ"], w=[bk])
                A("pe", lambda e, bank=bank, ci=ci: e.matmul(
                    bank[:, 0:128], lhsT=BKt[:C, ci, 128:256],
                    rhs=VZ[:C, ci, :].rearrange("p (b d) -> p b d", d=64)[:, 0::2, :],
                    start=False, stop=True, skip_group_check=True), r=["BKt", "VZ"], w=[bk])
                for e_ in range(2):
                    sl = slice(e_ * 64, e_ * 64 + 64)
                    A("dve", lambda e, bank=bank, sl=sl, ci=ci: e.scalar_tensor_tensor(
                        out=Sf[sl, sl], in0=Sf[sl, sl], scalar=gct[sl, ci:ci + 1], in1=bank[sl, sl],
                        op0=ALU.mult, op1=ALU.add), r=[bk, Sfk, "gct"], w=[Sfk])
                for e_ in range(2):
                    sl = slice(e_ * 64, e_ * 64 + 64)
                    A("pool", lambda e, sl=sl: e.tensor_copy(out=Sb[sl, sl], in_=Sf[sl, sl]), r=[Sfk], w=[Sbk])
                if sample:
                    state_store(si, p, 1 + ci)
                elif blk["tok0"] + 512 >= min(nblk, 8) * 512 and ci == nch - 1:
                    state_store(si, p, 0)

            bank, bk = mm.get()
            A("pe", lambda e, bank=bank: e.matmul(bank[:, :ntok], lhsT=blk64[:], rhs=F(YT), start=True, stop=True),
              r=["blk64", Fk(YT)], w=[bk])
            A("dve", lambda e, bank=bank: e.tensor_tensor(out=F(5), in0=F(YT), in1=bank[:, :ntok], op=ALU.subtract),
              r=[bk, Fk(YT)], w=[Fk(5)])
            A("act", lambda e: e.activation(out=F(6), in_=F(5), func=AF.Square), r=[Fk(5)], w=[Fk(6)])
            bank, bk = mm.get()
            A("pe", lambda e, bank=bank: e.matmul(bank[:, :ntok], lhsT=blk64[:], rhs=F(6), start=True, stop=True),
              r=["blk64", Fk(6)], w=[bk])
            A("act", lambda e, bank=bank: e.activation(out=F(6), in_=bank[:, :ntok], func=AF.Ln, bias=64e-5), r=[bk], w=[Fk(6)])
            A("act", lambda e: e.activation(out=F(6), in_=F(6), func=AF.Exp, scale=-0.5), r=[Fk(6)], w=[Fk(6)])
            A("dve", lambda e: e.tensor_tensor(out=F(5), in0=F(5), in1=F(6), op=ALU.mult), r=[Fk(5), Fk(6)], w=[Fk(5)])
            A("dve", lambda e: e.tensor_scalar(out=F(5), in0=F(5), scalar1=pc(5), scalar2=pc(6), op0=ALU.mult, op1=ALU.add),
              r=[Fk(5), "pvec"], w=[Fk(5)])
            A("dve", lambda e: e.scalar_tensor_tensor(out=v3(Fs[7]), in0=R, scalar=pc(4), in1=v3(Fs[3]), op0=ALU.mult,
                                                      op1=ALU.mult), r=[XPk, Fk(3), "pvec"], w=[Fk(7)])
            bank, bk = mm.get()
            A("pe", lambda e, bank=bank: e.matmul(bank[:, :ntok], lhsT=blk1[:], rhs=F(7), start=True, stop=True),
              r=["blk1", Fk(7)], w=[bk])
            A("dve", lambda e, bank=bank: e.tensor_tensor(
                out=v3(Fs[7]), in0=bank[:, :ntok].rearrange("p (s l) -> p s l", l=L), in1=VR, op=ALU.mult),
              r=[bk, XPk], w=[Fk(7)])
            A("dve", lambda e: e.tensor_tensor(out=F(5), in0=F(5), in1=F(7), op=ALU.add), r=[Fk(5), Fk(7)], w=[Fk(5)])
            ob, obk = orw.get()
            A("dve", lambda e, ob=ob: e.tensor_tensor(out=ob[:, :ntok], in0=F(5), in1=Gt[:, p, :ntok], op=ALU.mult),
              r=[Fk(5), "Gt"], w=[obk])
            if sample:
                for sq in range(2):
                    mix_store(ob[:, sq * NS:(sq + 1) * NS], obk, 2 + p, T + sq * NS, NS)
            else:
                mix_store(ob[:, :ntok], obk, 2 + p, blk["tok0"], ntok)

        blocks = []
        for tb in range(8):
            blocks.append(dict(tok0=tb * 512, ntok=512, sample=False,
                               tiles=[(tb * 512 + i * 128, 128) for i in range(4)]))
        blocks.append(dict(tok0=T, ntok=2 * NS, sample=True, tiles=[(T, NS), (T + NS, NS)]))
        nblk = int(os.environ.get("K_NBLK", "9"))
        do_attn = stage >= 2

        EVERY = int(os.environ.get("K_EVERY", "6"))
        todo = [b for i, b in enumerate(blocks) if stage > 0 and (b["sample"] or i < nblk)]
        if os.environ.get("K_NOSAMPLE"):
            todo = [b for b in todo if not b["sample"]]
        OVL = not os.environ.get("K_NOOVL")

        def part1(blk):
            hTt, hTk = project_block(blk)
            qTt, qTk = qT.get()
            proj_feature_major(blk, hTt, hTk, qTt, qTk, "qk")
            proj_token_major(blk, hTt, hTk)
            return (hTt, hTk, qTt, qTk)

        def part2(blk, hq):
            proj_feature_major(blk, hq[0], hq[1], hq[2], hq[3], "rw")

        def rwkv_entry(blk):
            sink[0] = []
            rwkv_block(blk)
            items = sink[0]
            sink[0] = None
            ym = os.environ.get("K_YMODE")
            if ym == "evac":
                nst = sum(1 for it in items if it[0] != "pe" and any(k in PSUM_KEYS for k in it[2])) // int(os.environ.get("K_EVMIN", "2")) + 1
                return [replay_items(items, EVERY, "evac"), nst]
            return [replay_items(items, EVERY), len(items) // EVERY + 1]

        def part1_entry(blk):
            sink[0] = []
            hq_ = part1(blk)
            items1 = sink[0]
            sink[0] = None
            ev1 = int(os.environ.get("K_EVERY_P1", "16"))
            return hq_, [replay_items(items1, ev1), len(items1) // ev1 + 1]

        prompts = [b for b in todo if not b["sample"]]
        samp = [b for b in todo if b["sample"]]
        samp = samp[0] if samp else None
        s_at = min(3, len(prompts) - 1) if (samp is not None and prompts) else None
        hq = None
        if prompts:
            hq = part1(prompts[0])
            part2(prompts[0], hq)
        elif samp is not None:
            hq_s = part1(samp)
            part2(samp, hq_s)
            interleave([rwkv_entry(samp)] if stage >= 3 else [] + ([[attention_sample(hq_s[2], hq_s[3]), 68]] if do_attn else []))
            if stage >= 3 and do_attn:
                interleave([[attention_sample(hq_s[2], hq_s[3]), 68]])
        for ti_, blk in enumerate(prompts):
            bi = blocks.index(blk)
            qTt, qTk = hq[2], hq[3]
            nxt = prompts[ti_ + 1] if ti_ + 1 < len(prompts) else None
            if s_at is not None and ti_ == s_at:
                hq_s = part1(samp)
                part2(samp, hq_s)
                ent1 = []
                if stage >= 3:
                    ent1.append(rwkv_entry(blk))
                if do_attn:
                    ent1.append([attention_sample(hq_s[2], hq_s[3]), 2 * 2 * 17])
                interleave(ent1)
                ent2 = []
                if do_attn:
                    ent2.append([attention_prompt(bi, qTt, qTk), int(16 * (bi + 1) * float(os.environ.get("K_ATTW", "1.4")))])
                if stage >= 3:
                    ent2.append(rwkv_entry(samp))
                hq_n = None
                if nxt is not None and OVL:
                    qT.get()
                    hq_n, e1 = part1_entry(nxt)
                    ent2.append(e1)
                interleave(ent2)
            else:
                entries = []
                if stage >= 3:
                    entries.append(rwkv_entry(blk))
                if do_attn:
                    entries.append([attention_prompt(bi, qTt, qTk), int(16 * (bi + 1) * float(os.environ.get("K_ATTW", "1.4")))])
                hq_n = None
                if nxt is not None and OVL:
                    hq_n, e1 = part1_entry(nxt)
                    entries.append(e1)
                interleave(entries)
            if nxt is not None:
                if hq_n is None:
                    hq_n = part1(nxt)
                part2(nxt, hq_n)
                hq = hq_n

        shst = sb("shst", [128, 3, 9])
        A("pool", lambda e: e.memset(shst[:].rearrange("p a b -> p (a b)"), 0.0), w=["shst"])
        lc_p, lc_s = (0, 0) if stage >= 3 else (512, NS)
        A("pool", lambda e: e.tensor_copy(out=shst[:, 0, :], in_=XPp[:, :, 0, lc_p]), r=["XP"], w=["shst"])
        for sq in range(2):
            A("pool", lambda e, sq=sq: e.tensor_copy(out=shst[:, 1 + sq, :], in_=XPs[:, :, sq, lc_s]),
              r=["XPs"], w=["shst"])
        out_toks.append(A(OUTQ, lambda e: e.dma_start(out=sho, in_=shst[:]), r=["shst"], dma="osh"))

        if dev and do_attn:
            pieces = [(0, min(nblk, 8) * 512), (T, 2 * NS)]
            nch = 4 if stage >= 3 else 2
            for (c0_, n_) in [pp for pp in pieces if pp[1] > 0]:
                out_toks.append(A("pool", lambda e, c0_=c0_, n_=n_: e.dma_start(
                    out=dbg_mix[0:nch * 128, c0_:c0_ + n_], in_=mixb_in[0:nch * 128, c0_:c0_ + n_]),
                    r=["mixb_in"], dma="dbg"))


        if stage >= 4:
            for c in range(4):
                A("pool", lambda e, c=c: e.collective_compute(
                    "AllGather", ALU.bypass, replica_groups=[[0, 1], [2, 3], [4, 5], [6, 7]],
                    ins=[mixb_in[c * 128:(c + 1) * 128, :]], outs=[mixb_outs[c][:, :]]),
                  r=["mixb_in"], w=["mixb_out"], cc=f"ag{c}")
            P.barrier()
            P.emit(st)
            stA.close()
            cur["st"] = st
            TB = 256
            gffn = sb("gffn", [128, D])
            gfin = sb("gfin", [128, D])
            A("sp", lambda e: e.dma_start(out=gffn[:], in_=g_ffn), w=["gffn"], dma="c0d")
            A("sp", lambda e: e.dma_start(out=gfin[:], in_=g_fin), w=["gfin"], dma="c0e")
            wout_bf = sb("wout_bf", [128, 8, D], BF16)
            wg_bf = sb("wg_bf", [128, 8, FF], BF16)
            wu_bf = sb("wu_bf", [128, 8, FF], BF16)
            wd_bf = sb("wd_bf", [128, 22, D], BF16)
            mA = sb("mA", [128, 8, TB], BF16)
            mB = sb("mB", [128, 8, TB], BF16)
            mS = mA
            xn = [sb(f"xn{i}", [128, D]) for i in range(2)]
            wstg = Rot("xn", xn)
            xbt = Rot("xbt", [sb(f"xbt{i}", [128, D]) for i in range(2)])
            h2b = Rot("h2b", [sb(f"h2b{i}", [128, D], BF16) for i in range(2)])
            h2T = sb("h2T", [128, 8, TB], BF16)
            hidT = sb("hidT", [128, 22, TB], BF16)
            sil = Rot("sil", [sb(f"sil{i}", [128, TB]) for i in range(2)])
            statb = Rot("statb", [sb(f"statb{i}", [128, 4]) for i in range(2)])
            ytile = Rot("ytile", [sb(f"ytile{i}", [128, D]) for i in range(1)])
            mmB = Rot("mmB", mm.bufs + Xb.bufs + oTb.bufs)
            mmB_keys = ["mm0", "mm1", "mm2", "X0", "X1", "oT0", "oT1"]

            def bankB():
                i = mmB.i % 7
                mmB.i += 1
                return mmB.bufs[i], mmB_keys[i]

            cast_engs = ["dve", "pool", "act"]
            cast_i = [0]

            def load_cast(dst_fn, src_ap_fn, nrow_chunks, ncols, wkey):
                for kc in range(nrow_chunks):
                    for c0 in range(0, ncols, 1024):
                        w_ = min(1024, ncols - c0)
                        stg, sk = wstg.get()
                        A("sp", lambda e, stg=stg, kc=kc, c0=c0, w_=w_: e.dma_start(
                            out=stg[:, 0:w_], in_=src_ap_fn(kc)[:, c0:c0 + w_]), w=[sk], dma=sk)
                        eng = cast_engs[cast_i[0] % 3]
                        cast_i[0] += 1
                        if eng == "act":
                            A("act", lambda e, stg=stg, kc=kc, c0=c0, w_=w_: e.activation(
                                out=dst_fn(kc)[:, c0:c0 + w_], in_=stg[:, 0:w_], func=AF.Copy), r=[sk], w=[wkey])
                        else:
                            A(eng, lambda e, stg=stg, kc=kc, c0=c0, w_=w_: e.tensor_copy(
                                out=dst_fn(kc)[:, c0:c0 + w_], in_=stg[:, 0:w_]), r=[sk], w=[wkey])

            def load_cast2(dst_fn, src_ap_fn, kcs, c0, c1, wkey):
                for kc in kcs:
                    A("pool", lambda e, kc=kc: e.dma_start(out=dst_fn(kc)[:, c0:c1], in_=src_ap_fn(kc)[:, c0:c1]),
                      w=[wkey], dma=wkey)

            NCB = 4
            CBW = FF // NCB
            load_cast2(lambda kc: wout_bf[:, kc, :], lambda kc: wout_d[kc * 128:(kc + 1) * 128, :], range(8), 0, D, "wout_bf")
            for cb in range(NCB):
                load_cast2(lambda kc: wg_bf[:, kc, :], lambda kc: wg_d[kc * 128:(kc + 1) * 128, :], range(8),
                           cb * CBW, (cb + 1) * CBW, f"wg_bf{cb}")
                load_cast2(lambda kc: wu_bf[:, kc, :], lambda kc: wu_d[kc * 128:(kc + 1) * 128, :], range(8),
                           cb * CBW, (cb + 1) * CBW, f"wu_bf{cb}")
            for ht in range(22):
                load_cast2(lambda kc: wd_bf[:, kc, :], lambda kc: wd_d[kc * 128:(kc + 1) * 128, :], [ht], 0, D, f"wd_bf{ht}")

            def wkeys(prefix, ht):
                a, b = (ht * 128) // CBW, (ht * 128 + 127) // CBW
                return [f"{prefix}{a}"] if a == b else [f"{prefix}{a}", f"{prefix}{b}"]

            mos = [mixb_outs[c][:, :].rearrange("(r p) t -> p r t", p=128) for c in range(4)]

            def norm_tile(src, srck, nrows, gt, gk, dst, dstk):
                stt, sk = statb.get()
                A("act", lambda e: e.activation(out=dst[:nrows, :], in_=src[:nrows, :], func=AF.Square,
                                                accum_out=stt[:nrows, 0:1]), r=[srck], w=[dstk, sk])
                A("act", lambda e: e.activation(out=stt[:nrows, 1:2], in_=stt[:nrows, 0:1], func=AF.Ln,
                                                scale=1.0 / D, bias=1e-6), r=[sk], w=[sk])
                A("act", lambda e: e.activation(out=stt[:nrows, 2:3], in_=stt[:nrows, 1:2], func=AF.Exp,
                                                scale=-0.5), r=[sk], w=[sk])
                A("dve", lambda e: e.scalar_tensor_tensor(out=dst[:nrows, :], in0=src[:nrows, :],
                                                          scalar=stt[:nrows, 2:3], in1=gt[:nrows, :],
                                                          op0=ALU.mult, op1=ALU.mult), r=[srck, sk, gk], w=[dstk])

            bblocks = [(i * TB, TB, i * TB, T // 2 + i * TB) for i in range(T // 2 // TB)]
            bblocks.append((T // 2, NS, T, T + NS))
            h2bs = h2b.bufs

            def tiles_of(ntok):
                return [(t0, min(128, ntok - t0)) for t0 in range(0, ntok, 128)]

            def prep(j):
                row0, ntok, colA, colB = bblocks[j]
                for c in range(4):
                    A("sp", lambda e, c=c, colA=colA, ntok=ntok: e.dma_start(
                        out=mA[:, c::4, :ntok], in_=mos[c][:, :, colA:colA + ntok]), r=["mixb_out"], w=["mA"], dma="mA")
                    A("sp", lambda e, c=c, colB=colB, ntok=ntok: e.dma_start(
                        out=mB[:, c::4, :ntok], in_=mos[c][:, :, colB:colB + ntok]), r=["mixb_out"], w=["mB"], dma="mB")
                A("dve", lambda e, ntok=ntok: e.tensor_scalar(out=mA[:, :, :ntok], in0=mA[:, :, :ntok],
                                                              scalar1=rsel[:, 0:1], scalar2=None, op0=ALU.mult),
                  r=["mA", "rsel"], w=["mA"])
                A("dve", lambda e, ntok=ntok: e.scalar_tensor_tensor(out=mA[:, :, :ntok], in0=mB[:, :, :ntok],
                                                                      scalar=rsel[:, 1:2], in1=mA[:, :, :ntok],
                                                                      op0=ALU.mult, op1=ALU.add),
                  r=["mA", "mB", "rsel"], w=["mA"])

            def s1(j):
                row0, ntok, colA, colB = bblocks[j]
                tl = tiles_of(ntok)
                xbs = []
                for ti, (t0, nr) in enumerate(tl):
                    xb_t, xbk = xbt.get()
                    A("sp", lambda e, xb_t=xb_t, t0=t0, nr=nr, row0=row0: e.dma_start(
                        out=xb_t[:nr, :], in_=xb[row0 + t0:row0 + t0 + nr, :]), w=[xbk], dma=xbk)
                    xbs.append((xb_t, xbk))
                banks = []
                for ti, (t0, nr) in enumerate(tl):
                    for hf in range(2):
                        bank, bk = bankB()
                        for kc in range(8):
                            A("pe", lambda e, bank=bank, kc=kc, hf=hf, t0=t0, nr=nr: e.matmul(
                                bank[:nr, :], lhsT=mA[:, kc, t0:t0 + nr], rhs=wout_bf[:, kc, hf * 512:(hf + 1) * 512],
                                start=(kc == 0), stop=(kc == 7)), r=["mA", "wout_bf"], w=[bk])
                        banks.append((bank, bk))
                for ti, (t0, nr) in enumerate(tl):
                    xnt, xnk = xn[ti], f"xn{ti}"
                    xb_t, xbk = xbs[ti]
                    for hf in range(2):
                        bank, bk = banks[ti * 2 + hf]
                        A("dve", lambda e, bank=bank, hf=hf, xnt=xnt, xb_t=xb_t, nr=nr: e.tensor_tensor(
                            out=xnt[:nr, hf * 512:(hf + 1) * 512], in0=bank[:nr, :], in1=xb_t[:nr, hf * 512:(hf + 1) * 512],
                            op=ALU.add), r=[bk, xbk], w=[xnk])
                    norm_tile(xnt, xnk, nr, gffn, "gffn", h2bs[ti], f"h2b{ti}")

            def s1b(j):
                row0, ntok, colA, colB = bblocks[j]
                for ti, (t0, nr) in enumerate(tiles_of(ntok)):
                    hb2, hbk2 = h2bs[ti], f"h2b{ti}"
                    for kc in range(8):
                        A("pe", lambda e, kc=kc, hb2=hb2, nr=nr: e.transpose(
                            out=tpb[:, kc * 128:kc * 128 + nr], in_=hb2[:nr, kc * 128:(kc + 1) * 128],
                            identity=ident_bf[:nr, :nr]), r=[hbk2, "ident_bf"], w=["tpb"])
                    A("act", lambda e, t0=t0, nr=nr: e.activation(
                        out=h2T[:, :, t0:t0 + nr], in_=tpb[:].rearrange("p (k t) -> p k t", t=128)[:, :, 0:nr],
                        func=AF.Copy), r=["tpb"], w=["h2T"])

            def s2(j):
                row0, ntok, colA, colB = bblocks[j]
                for ht in range(22):
                    hs_ = slice(ht * 128, (ht + 1) * 128)
                    gb, gbk = bankB()
                    for kc in range(8):
                        A("pe", lambda e, gb=gb, kc=kc, hs_=hs_, ntok=ntok: e.matmul(
                            gb[:, :ntok], lhsT=wg_bf[:, kc, hs_], rhs=h2T[:, kc, :ntok], start=(kc == 0), stop=(kc == 7)),
                          r=wkeys("wg_bf", ht) + ["h2T"], w=[gbk])
                    ub, ubk = bankB()
                    for kc in range(8):
                        A("pe", lambda e, ub=ub, kc=kc, hs_=hs_, ntok=ntok: e.matmul(
                            ub[:, :ntok], lhsT=wu_bf[:, kc, hs_], rhs=h2T[:, kc, :ntok], start=(kc == 0), stop=(kc == 7)),
                          r=wkeys("wu_bf", ht) + ["h2T"], w=[ubk])
                    st_, stk_ = sil.get()
                    A("act", lambda e, st_=st_, gb=gb, ntok=ntok: e.activation(out=st_[:, :ntok], in_=gb[:, :ntok],
                                                                               func=AF.Silu), r=[gbk], w=[stk_])
                    A("dve", lambda e, st_=st_, ub=ub, ht=ht, ntok=ntok: e.tensor_tensor(
                        out=hidT[:, ht, :ntok], in0=ub[:, :ntok], in1=st_[:, :ntok], op=ALU.mult),
                      r=[ubk, stk_], w=["hidT"])

            def s3(j):
                row0, ntok, colA, colB = bblocks[j]
                for ti, (t0, nr) in enumerate(tiles_of(ntok)):
                    xnt, xnk = xn[ti], f"xn{ti}"
                    for hf in range(2):
                        bank, bk = bankB()
                        for ht in range(22):
                            A("pe", lambda e, bank=bank, ht=ht, hf=hf, t0=t0, nr=nr: e.matmul(
                                bank[:nr, :], lhsT=hidT[:, ht, t0:t0 + nr], rhs=wd_bf[:, ht, hf * 512:(hf + 1) * 512],
                                start=(ht == 0), stop=(ht == 21)), r=["hidT", f"wd_bf{ht}"], w=[bk])
                        A("dve", lambda e, bank=bank, hf=hf, xnt=xnt, nr=nr: e.tensor_tensor(
                            out=xnt[:nr, hf * 512:(hf + 1) * 512], in0=bank[:nr, :], in1=xnt[:nr, hf * 512:(hf + 1) * 512],
                            op=ALU.add), r=[bk, xnk], w=[xnk])
                    yt, ytk = ytile.get()
                    norm_tile(xnt, xnk, nr, gfin, "gfin", yt, ytk)
                    out_toks.append(A(OUTQ, lambda e, yt=yt, t0=t0, nr=nr, row0=row0: e.dma_start(
                        out=yo[row0 + t0:row0 + t0 + nr, :], in_=yt[:nr, :]), r=[ytk], dma="o" + ytk))

            nbb = len(bblocks)
            prep(0)
            for j in range(nbb):
                s1(j)
                if j + 1 < nbb:
                    prep(j + 1)
                s1b(j)
                s2(j)
                s3(j)

        P.wait_all(OUTQ, [t.tok if isinstance(t, PH) else t for t in out_toks])
        P.barrier()
        P.emit(st)
        if stage < 4:
            stA.close()
    return nc


def _core_inputs(inp, c):
    b, j = c // 2, c % 2
    f = lambda a: np.ascontiguousarray(a, dtype=np.float32)
    x_prompt, x_sample = inp["x_prompt"], inp["x_sample"]
    xa = np.concatenate([x_prompt[b], x_sample[2 * b], x_sample[2 * b + 1]], axis=0)
    xb = np.concatenate([x_prompt[b, j * 2048:(j + 1) * 2048], x_sample[2 * b + j]], axis=0)
    hs = slice(j * 256, (j + 1) * 256)
    w_in = inp["w_in"][0]
    sbw = 512
    rw0 = 3 * sbw
    cols = np.concatenate([
        np.arange(0, 512)[hs], 512 + np.arange(0, 512)[hs], 1024 + np.arange(0, 512)[hs],
        rw0 + np.arange(0, 512)[hs], rw0 + 512 + np.arange(0, 512)[hs], rw0 + 1024 + np.arange(0, 512)[hs],
        rw0 + 1536 + np.arange(0, 288)])
    win = w_in[:, cols]
    rcols = cols[768:] - rw0
    mu = inp["mu_shift"][0][rcols]

    def tiles9(v):
        o = np.zeros((128, 9), np.float32)
        for i in range(8):
            o[:, i] = v[i * 128:(i + 1) * 128]
        o[:32, 8] = v[1024:1056]
        return o

    pvec = np.zeros((128, 24), np.float32)
    pvec[:, 0:9] = tiles9(mu)
    names = ["w0", "a0", "k_k", "k_a", "r_k", "ln_x_w", "ln_x_b"]
    for p in range(2):
        for i, n in enumerate(names):
            v = inp[n][0].reshape(-1)[hs]
            pvec[:, 9 + p * 7 + i] = v[p * 128:(p + 1) * 128]
    wlo = np.concatenate([inp["w_decay_up"][0][:, hs], inp["w_aaa_up"][0][:, hs]], axis=0)
    wgu = inp["w_gate_up"][0][:, hs]
    ck = inp["cache_k"][0][2 * b:2 * b + 2, :, 4 * j:4 * j + 4, :].reshape(2, 2048, 256)
    cv = inp["cache_v"][0][2 * b:2 * b + 2, :, 4 * j:4 * j + 4, :].reshape(2, 2048, 256)
    swkv = inp["state_wkv"][0][2 * b:2 * b + 2, 4 * j:4 * j + 4]
    ssh = inp["state_shift"][0][2 * b:2 * b + 2, 0][:, rcols]
    sshift = np.stack([tiles9(ssh[0]), tiles9(ssh[1])], axis=1)
    perm = []
    for jj in range(2):
        for kind in range(2):
            for p in range(2):
                perm.append(kind * 512 + jj * 256 + p * 128 + np.arange(128))
    perm = np.concatenate(perm)
    wout = inp["w_out"][0][perm, :]
    rsel = np.zeros((128, 2), np.float32)
    rsel[:, j] = 1.0
    rep = lambda v: np.broadcast_to(v.reshape(1, -1), (128, v.size))
    return {
        "xa": f(xa), "xb": f(xb), "win": f(win),
        "g_mix": f(rep(inp["norm_mix_g"][0])), "g_ffn": f(rep(inp["norm_ffn_g"][0])),
        "g_fin": f(rep(inp["norm_final_g"])),
        "pvec": f(pvec), "wlo": f(wlo), "wgu": f(wgu), "ck": f(ck), "cv": f(cv),
        "swkv": f(swkv), "sshift": f(sshift), "wout": f(wout),
        "wg": f(inp["w_gate"][0]), "wu": f(inp["w_up"][0]), "wd": f(inp["w_down"][0]),
        "rsel": f(rsel),
    }


_NC_CACHE = {}


def run(inputs, stage=int(os.environ.get("K_STAGE_DEFAULT", "4")), dev=False):
    inp = {k: np.asarray(v) for k, v in inputs.items()}
    key = (stage, dev)
    if key not in _NC_CACHE:
        _NC_CACHE[key] = build(stage, dev)
    nc = _NC_CACHE[key]
    in_maps = [_core_inputs(inp, c) for c in range(8)]
    res = run_bass_kernel_spmd(nc, in_maps, core_ids=list(range(8)))
    return res.results


def kernel(**inputs):
    r = run(inputs)
    B, S, H, Dh = 4, T, 8, 64
    y_prompt = np.zeros((B, S, D), np.float32)
    y_sample = np.zeros((8, NS, D), np.float32)
    pk = np.zeros((1, B, S, H, Dh), np.float32)
    pv = np.zeros_like(pk)
    pS = np.zeros((1, B, H, Dh, Dh), np.float32)
    psh = np.zeros((1, B, 1, 1824), np.float32)
    sk = np.zeros((1, 8, NS, H, Dh), np.float32)
    sv = np.zeros_like(sk)
    sS = np.zeros((1, 8, H, Dh, Dh), np.float32)
    ssh = np.zeros((1, 8, 1, 1824), np.float32)
    for c in range(8):
        b, j = c // 2, c % 2
        o = r[c]
        hs = slice(4 * j, 4 * j + 4)
        pk[0, b, :, hs, :] = o["ko"][:T].reshape(T, 4, Dh)
        pv[0, b, :, hs, :] = o["vo"][:T].reshape(T, 4, Dh)
        for s in range(2):
            rows = slice(T + s * NS, T + (s + 1) * NS)
            sk[0, 2 * b + s, :, hs, :] = o["ko"][rows].reshape(NS, 4, Dh)
            sv[0, 2 * b + s, :, hs, :] = o["vo"][rows].reshape(NS, 4, Dh)
        y_prompt[b, j * 2048:(j + 1) * 2048] = o["yo"][:2048]
        y_sample[2 * b + j] = o["yo"][2048:]
        pS[0, b, hs] = o["wkvo"][0]
        sS[0, 2 * b, hs] = o["wkvo"][1]
        sS[0, 2 * b + 1, hs] = o["wkvo"][2]
        sho = o["sho"]
        for which, (dst, bi) in enumerate([(psh, b), (ssh, 2 * b), (ssh, 2 * b + 1)]):
            t9 = sho[:, which, :]
            for i, base in enumerate([0, 512, 1024]):
                for p in range(2):
                    dst[0, bi, 0, base + j * 256 + p * 128: base + j * 256 + (p + 1) * 128] = t9[:, 2 * i + p]
            dst[0, bi, 0, 1536:1664] = t9[:, 6]
            dst[0, bi, 0, 1664:1792] = t9[:, 7]
            dst[0, bi, 0, 1792:1824] = t9[:32, 8]
    return (y_prompt, y_sample, pk, pv, pS, psh, sk, sv, sS, ssh)
```
